# Optimizing a Trainium2 kernel written in Bass

```python
import jax, jax.numpy as jnp
from jax import lax
import numpy as np

D_MODEL = 4096
BATCH = 1
SEQ = 8192
DEPTH = 1
DEC_BATCH = 16
DEC_SEQ = 64
PAST_LEN = 1024

CHUNK = 64
EPS = 1e-6
MIX_DIM = D_MODEL
V_HEAD_DIM = 128
MLA_HEADS = (MIX_DIM // 2) // V_HEAD_DIM
QK_NOPE = 128
ROPE_DIM = 64
QK_DIM = QK_NOPE + ROPE_DIM
Q_LORA = 1024
KV_LORA = 512
ROPE_THETA = 10000.0
SOFTMAX_SCALE = QK_DIM ** -0.5
Q_BLOCK = 128
POOL_DIM = MIX_DIM - MLA_HEADS * V_HEAD_DIM
POOL_WINDOWS = (2, 4, 8, 16)
POOL_GROUPS = len(POOL_WINDOWS)
POOL_GC = POOL_DIM // POOL_GROUPS
POOL_HIST = max(POOL_WINDOWS) - 1
IN_DIM = Q_LORA + KV_LORA + ROPE_DIM + POOL_DIM
OFF_KV = Q_LORA
OFF_PE = Q_LORA + KV_LORA
OFF_POOL = Q_LORA + KV_LORA + ROPE_DIM
PEER_HEADS = 8
PEER_N_KEYS = 128
PEER_EXPERTS = PEER_N_KEYS * PEER_N_KEYS
PEER_KEY_DIM = 256
PEER_HALF = PEER_KEY_DIM // 2
PEER_TOPK = 16
PEER_BLOCK = 32

kernel_name = "hybrid_mla_pool_peer_stream_step"


def rmsnorm(x, g):
    xf = x.astype(jnp.float32)
    r = lax.rsqrt(jnp.mean(xf * xf, axis=-1, keepdims=True) + EPS)
    return (xf * r * g.astype(jnp.float32)).astype(x.dtype)


def rope(x, pos):
    half = ROPE_DIM // 2
    inv = ROPE_THETA ** (-2.0 * jnp.arange(half, dtype=jnp.float32) / ROPE_DIM)
    ang = pos.astype(jnp.float32)[:, None] * inv[None, :]
    shape = (1, pos.shape[0]) + (1,) * (x.ndim - 3) + (half,)
    cos = jnp.cos(ang).reshape(shape)
    sin = jnp.sin(ang).reshape(shape)
    xf = x.astype(jnp.float32)
    x1, x2 = xf[..., :half], xf[..., half:]
    return jnp.concatenate([x1 * cos - x2 * sin, x1 * sin + x2 * cos], axis=-1).astype(x.dtype)


def chunk_attn(q_nope, q_pe, k_nope, k_pe, v, q_pos, k_pos):
    s = (jnp.einsum('bqhd,bkhd->bhqk', q_nope, k_nope)
         + jnp.einsum('bqhr,bkr->bhqk', q_pe, k_pe)).astype(jnp.float32) * SOFTMAX_SCALE
    mask = (q_pos[:, None] // CHUNK) >= (k_pos[None, :] // CHUNK)
    s = jnp.where(mask[None, None], s, -jnp.inf)
    p = jax.nn.softmax(s, axis=-1)
    return jnp.einsum('bhqk,bkhd->bqhd', p.astype(v.dtype), v)


def mla_attention(q_nope, q_pe, k_nope, k_pe, v, q_pos, k_pos):
    B, T = q_nope.shape[0], q_nope.shape[1]
    if T > Q_BLOCK and T % Q_BLOCK == 0:
        nblk = T // Q_BLOCK
        qn = q_nope.reshape(B, nblk, Q_BLOCK, MLA_HEADS, QK_NOPE).transpose(1, 0, 2, 3, 4)
        qp = q_pe.reshape(B, nblk, Q_BLOCK, MLA_HEADS, ROPE_DIM).transpose(1, 0, 2, 3, 4)
        pb = q_pos.reshape(nblk, Q_BLOCK)
        o = lax.map(lambda a: chunk_attn(a[0], a[1], k_nope, k_pe, v, a[2], k_pos), (qn, qp, pb))
        o = o.transpose(1, 0, 2, 3, 4)
    else:
        o = chunk_attn(q_nope, q_pe, k_nope, k_pe, v, q_pos, k_pos)
    return o.reshape(B, T, MLA_HEADS * V_HEAD_DIM)


def pool_mixer(u_ext, abs_pos, T, pool_w, pool_scale):
    B, L, _ = u_ext.shape
    uf = u_ext.astype(jnp.float32)
    cs = jnp.cumsum(uf, axis=1)
    parts = []
    for g, w in enumerate(POOL_WINDOWS):
        c = cs[..., g * POOL_GC:(g + 1) * POOL_GC]
        lagged = jnp.pad(c, ((0, 0), (w, 0), (0, 0)))[:, :L]
        cnt = jnp.minimum(abs_pos + 1, w).astype(jnp.float32)[None, :, None]
        mean = (c - lagged) / cnt
        parts.append(mean[:, L - T:] - uf[:, L - T:, g * POOL_GC:(g + 1) * POOL_GC])
    d = jnp.stack(parts, axis=2)
    out = jnp.einsum('btgc,gce->btge', d, pool_w.astype(jnp.float32)).reshape(B, T, POOL_DIM)
    return (out * pool_scale.astype(jnp.float32)).astype(u_ext.dtype)


def peer_ffn(h, wq, sk1, sk2, u, v):
    B, T, D = h.shape
    q = jnp.einsum('btd,dhk->bthk', h, wq)
    s1 = jnp.einsum('bthk,hnk->bthn', q[..., :PEER_HALF], sk1).astype(jnp.float32)
    s2 = jnp.einsum('bthk,hnk->bthn', q[..., PEER_HALF:], sk2).astype(jnp.float32)
    v1, i1 = lax.top_k(s1, PEER_TOPK)
    v2, i2 = lax.top_k(s2, PEER_TOPK)
    nc = PEER_TOPK * PEER_TOPK
    cand = (v1[..., :, None] + v2[..., None, :]).reshape(B, T, PEER_HEADS, nc)
    cidx = (i1[..., :, None] * PEER_N_KEYS + i2[..., None, :]).reshape(B, T, PEER_HEADS, nc)
    top, sel = lax.top_k(cand, PEER_TOPK)
    idx = jnp.take_along_axis(cidx, sel, axis=-1)
    gate = jax.nn.softmax(top, axis=-1)
    n = B * T
    K = PEER_HEADS * PEER_TOPK
    nb = -(-n // PEER_BLOCK)
    pad = nb * PEER_BLOCK - n
    hf = jnp.pad(h.reshape(n, D), ((0, pad), (0, 0))).reshape(nb, PEER_BLOCK, D)
    idf = jnp.pad(idx.reshape(n, K), ((0, pad), (0, 0))).reshape(nb, PEER_BLOCK, K)
    gf = jnp.pad(gate.reshape(n, K), ((0, pad), (0, 0))).reshape(nb, PEER_BLOCK, K)

    def expert_block(args):
        hb, ib, gb = args
        a = jnp.einsum('nd,nkd->nk', hb, jnp.take(u, ib, axis=0)).astype(jnp.float32)
        act = (jax.nn.gelu(a, approximate=False) * gb).astype(hb.dtype)
        return jnp.einsum('nk,nkd->nd', act, jnp.take(v, ib, axis=0))

    y = lax.map(expert_block, (hf, idf, gf))
    return y.reshape(nb * PEER_BLOCK, D)[:n].reshape(B, T, D).astype(h.dtype)


def hybrid_layer(x, pos, ckv_hist, kpe_hist, pool_hist, ln1_g, w_in, q_norm_g, w_uq, kv_norm_g,
                 w_ukv, pool_w, pool_scale, w_o, ln2_g, peer_wq, peer_sk1, peer_sk2, peer_u, peer_v):
    B, T, _ = x.shape
    h = rmsnorm(x, ln1_g)
    z = jnp.einsum('btd,de->bte', h, w_in)
    zq, zkv = z[..., :OFF_KV], z[..., OFF_KV:OFF_PE]
    zpe, zpool = z[..., OFF_PE:OFF_POOL], z[..., OFF_POOL:]
    q = jnp.einsum('btc,che->bthe', rmsnorm(zq, q_norm_g), w_uq)
    q_nope, q_pe = q[..., :QK_NOPE], rope(q[..., QK_NOPE:], pos)
    ckv_new = rmsnorm(zkv, kv_norm_g)
    kpe_new = rope(zpe, pos)
    if ckv_hist is None:
        ckv_all, kpe_all, k_pos = ckv_new, kpe_new, pos
    else:
        ckv_all = jnp.concatenate([ckv_hist.astype(ckv_new.dtype), ckv_new], axis=1)
        kpe_all = jnp.concatenate([kpe_hist.astype(kpe_new.dtype), kpe_new], axis=1)
        k_pos = jnp.arange(ckv_all.shape[1], dtype=jnp.int32)
    kv = jnp.einsum('bsc,che->bshe', ckv_all, w_ukv)
    k_nope, v = kv[..., :QK_NOPE], kv[..., QK_NOPE:]
    o_mla = mla_attention(q_nope, q_pe, k_nope, kpe_all, v, pos, k_pos)
    if pool_hist is None:
        u_ext, abs_pos = zpool, pos
    else:
        u_ext = jnp.concatenate([pool_hist.astype(zpool.dtype), zpool], axis=1)
        abs_pos = pos[0] - POOL_HIST + jnp.arange(POOL_HIST + T, dtype=jnp.int32)
    o_pool = pool_mixer(u_ext, abs_pos, T, pool_w, pool_scale)
    pool_new = u_ext[:, -POOL_HIST:]
    o = jnp.concatenate([o_mla.astype(x.dtype), o_pool.astype(x.dtype)], axis=-1)
    x = x + jnp.einsum('btm,md->btd', o, w_o).astype(x.dtype)
    x = x + peer_ffn(rmsnorm(x, ln2_g), peer_wq, peer_sk1, peer_sk2, peer_u, peer_v)
    return x, ckv_new, kpe_new, pool_new


def setup_inputs(seed: int = 0) -> dict:
    key = jax.random.key(seed)
    ks = jax.random.split(key, 24)

    def nrm(k, shape, scale):
        return jax.random.normal(k, shape, jnp.float32) * scale

    return {
        "x_prompt": nrm(ks[0], (BATCH, SEQ, D_MODEL), 1.0),
        "x_sample": nrm(ks[1], (DEC_BATCH, DEC_SEQ, D_MODEL), 1.0),
        "cache_ckv": nrm(ks[2], (DEPTH, DEC_BATCH, PAST_LEN, KV_LORA), 1.0),
        "cache_kpe": nrm(ks[3], (DEPTH, DEC_BATCH, PAST_LEN, ROPE_DIM), 1.0),
        "state_pool": nrm(ks[4], (DEPTH, DEC_BATCH, POOL_HIST, POOL_DIM), 1.0),
        "ln1_g": 1.0 + nrm(ks[5], (DEPTH, D_MODEL), 0.02),
        "w_in": nrm(ks[6], (DEPTH, D_MODEL, IN_DIM), D_MODEL ** -0.5),
        "q_norm_g": 1.0 + nrm(ks[7], (DEPTH, Q_LORA), 0.02),
        "w_uq": nrm(ks[8], (DEPTH, Q_LORA, MLA_HEADS, QK_DIM), Q_LORA ** -0.5),
        "kv_norm_g": 1.0 + nrm(ks[9], (DEPTH, KV_LORA), 0.02),
        "w_ukv": nrm(ks[10], (DEPTH, KV_LORA, MLA_HEADS, QK_NOPE + V_HEAD_DIM), KV_LORA ** -0.5),
        "pool_w": nrm(ks[11], (DEPTH, POOL_GROUPS, POOL_GC, POOL_GC), POOL_GC ** -0.5),
        "pool_scale": 1.0 + nrm(ks[12], (DEPTH, POOL_DIM), 0.02),
        "w_o": nrm(ks[13], (DEPTH, MIX_DIM, D_MODEL), MIX_DIM ** -0.5),
        "ln2_g": 1.0 + nrm(ks[14], (DEPTH, D_MODEL), 0.02),
        "peer_wq": nrm(ks[15], (DEPTH, D_MODEL, PEER_HEADS, PEER_KEY_DIM), D_MODEL ** -0.5),
        "peer_sk1": nrm(ks[16], (DEPTH, PEER_HEADS, PEER_N_KEYS, PEER_HALF), PEER_HALF ** -0.5),
        "peer_sk2": nrm(ks[17], (DEPTH, PEER_HEADS, PEER_N_KEYS, PEER_HALF), PEER_HALF ** -0.5),
        "peer_u": nrm(ks[18], (DEPTH, PEER_EXPERTS, D_MODEL), D_MODEL ** -0.5),
        "peer_v": nrm(ks[19], (DEPTH, PEER_EXPERTS, D_MODEL), (PEER_HEADS * PEER_TOPK) ** -0.5),
        "final_g": 1.0 + nrm(ks[20], (D_MODEL,), 0.02),
    }


def reference(x_prompt, x_sample, cache_ckv, cache_kpe, state_pool, ln1_g, w_in, q_norm_g, w_uq,
              kv_norm_g, w_ukv, pool_w, pool_scale, w_o, ln2_g, peer_wq, peer_sk1, peer_sk2,
              peer_u, peer_v, final_g):
    xp, xs = x_prompt, x_sample
    past = cache_ckv.shape[2]
    pos_p = jnp.arange(xp.shape[1], dtype=jnp.int32)
    pos_s = past + jnp.arange(xs.shape[1], dtype=jnp.int32)
    ckv_p, kpe_p, pool_p, ckv_s, kpe_s, pool_s = [], [], [], [], [], []
    for l in range(DEPTH):
        w = (ln1_g[l], w_in[l], q_norm_g[l], w_uq[l], kv_norm_g[l], w_ukv[l], pool_w[l],
             pool_scale[l], w_o[l], ln2_g[l], peer_wq[l], peer_sk1[l], peer_sk2[l],
             peer_u[l], peer_v[l])
        xp, c1, k1, p1 = hybrid_layer(xp, pos_p, None, None, None, *w)
        xs, c2, k2, p2 = hybrid_layer(xs, pos_s, cache_ckv[l], cache_kpe[l], state_pool[l], *w)
        ckv_p.append(c1); kpe_p.append(k1); pool_p.append(p1)
        ckv_s.append(c2); kpe_s.append(k2); pool_s.append(p2)
    y_prompt = rmsnorm(xp, final_g)
    y_sample = rmsnorm(xs, final_g)
    return (y_prompt, y_sample, jnp.stack(ckv_p), jnp.stack(kpe_p), jnp.stack(pool_p),
            jnp.stack(ckv_s), jnp.stack(kpe_s), jnp.stack(pool_s))
```

```python
import numpy as np
import concourse.bass as bass
import concourse.mybir as mybir
from concourse.bass_utils import run_bass_kernel_spmd

F32 = mybir.dt.float32
BF16 = mybir.dt.bfloat16
U8 = mybir.dt.uint8
AF = mybir.ActivationFunctionType
ALU = mybir.AluOpType
AX = mybir.AxisListType

NCORES = 8
D = 4096
EPS = 1e-6
SCALE = 192 ** -0.5
TE = 1280
TC = 1152
TU = 1312
PIECES_TE = ((0, 512), (512, 1024), (1024, 1280))
PIECES_TC = ((0, 512), (512, 1024), (1024, 1152))
ENGS = ("pe", "dve", "act", "pool", "sp")


class Prog:
    def __init__(self, nc, same_engine_sync=True):
        self.nc = nc
        self.ops = []
        self.last_w = {}
        self.readers = {}
        self.same_engine_sync = same_engine_sync
        self.last_eng = {}
        self.dma_since = {}
        self.pending = {}
        self.slots = {}

    def op(self, eng, fn, reads=(), writes=(), dma=False, semkey=None):
        i = len(self.ops)
        deps = set()
        for r in reads:
            if r in self.last_w:
                deps.add(self.last_w[r])
        for w in writes:
            if w in self.last_w:
                deps.add(self.last_w[w])
            for rd in self.readers.get(w, ()):
                deps.add(rd)
        if eng in self.pending:
            deps |= self.pending.pop(eng)
        deps.discard(i)
        for r in reads:
            lst = self.readers.setdefault(r, [])
            if not dma and lst and (not self.ops[lst[-1]]["dma"]) and self.ops[lst[-1]]["eng"] == eng:
                lst[-1] = i
            else:
                lst.append(i)
        for w in writes:
            self.last_w[w] = i
            self.readers[w] = []
        if dma:
            if semkey is None:
                semkey = ("dma",) + tuple(writes) + tuple(reads)
            semkey = self.slots.setdefault(semkey, len(self.slots))
            self.dma_since[semkey] = i
        else:
            self.last_eng[eng] = i
        self.ops.append(dict(eng=eng, fn=fn, dma=dma, semkey=semkey, deps=deps))
        return i

    def barrier(self):
        deps = set(self.last_eng.values()) | set(self.dma_since.values())
        for e in ENGS:
            self.pending[e] = set(deps) | self.pending.get(e, set())
        self.dma_since = {}
        self.last_w = {}
        self.readers = {}
        self.slots = {}

    def emit(self, final_wait_ops=()):
        nc = self.nc
        ops = self.ops

        def skip(p, o):
            if p["dma"] or o["dma"]:
                return False
            if p["eng"] != o["eng"]:
                return False
            return p["eng"] == "pe" or not self.same_engine_sync

        needed = set()
        for i, o in enumerate(ops):
            for d in o["deps"]:
                if not skip(ops[d], o):
                    needed.add(d)
        for d in final_wait_ops:
            needed.add(d)
        eng_cnt = {e: 0 for e in ENGS}
        dma_cnt = {}
        sig = {}
        dma_keys = []
        for i, o in enumerate(ops):
            if o["dma"]:
                k = o["semkey"]
                if k not in dma_cnt:
                    dma_cnt[k] = 0
                    dma_keys.append(k)
                dma_cnt[k] += 16
                sig[i] = (("dma", k), dma_cnt[k])
            elif i in needed:
                eng_cnt[o["eng"]] += 1
                sig[i] = (("eng", o["eng"]), eng_cnt[o["eng"]])
        sems = {}
        for e in ENGS:
            sems[("eng", e)] = nc.alloc_semaphore(name=f"s_{e}")
        for n, k in enumerate(dma_keys):
            sems[("dma", k)] = nc.alloc_semaphore(name=f"d_{n}")
        self.n_sems = len(sems)
        per_eng = {e: [] for e in ENGS}
        for i, o in enumerate(ops):
            per_eng[o["eng"]].append(i)
        waited = {e: {} for e in ENGS}

        def run(e, h):
            wd = waited[e]
            for i in per_eng[e]:
                o = ops[i]
                req = {}
                for d in o["deps"]:
                    if d not in sig or skip(ops[d], o):
                        continue
                    sk, v = sig[d]
                    if v > req.get(sk, 0):
                        req[sk] = v
                for sk, v in req.items():
                    if wd.get(sk, 0) >= v:
                        continue
                    h.wait_ge(sems[sk], v)
                    wd[sk] = v
                ins = o["fn"](h)
                if i in sig:
                    sk, v = sig[i]
                    ins.then_inc(sems[sk], 16 if o["dma"] else 1)
            if e == "sp":
                for d in final_wait_ops:
                    sk, v = sig[d]
                    if wd.get(sk, 0) < v:
                        h.wait_ge(sems[sk], v)
                        wd[sk] = v

        with nc.Block() as block:
            @block.tensor
            def _(h):
                run("pe", h)

            @block.vector
            def _(h):
                run("dve", h)

            @block.scalar
            def _(h):
                run("act", h)

            @block.gpsimd
            def _(h):
                run("pool", h)

            @block.sync
            def _(h):
                run("sp", h)
        return eng_cnt


ISZ = {F32: 4, BF16: 2, U8: 1}


class Arena:
    def __init__(self, nc, nbytes):
        self.t = nc.alloc_sbuf_tensor("arena", [128, nbytes], U8)
        self.nbytes = nbytes
        self.lo = 0
        self.hi = nbytes

    def _view(self, off, shape, dtype):
        n = int(np.prod(shape))
        ap = self.t[:, off:off + n * ISZ[dtype]].bitcast(dtype)
        if len(shape) == 1:
            return ap
        names = " ".join(f"d{i}" for i in range(len(shape)))
        kw = {f"d{i}": int(s) for i, s in enumerate(shape)}
        return ap.rearrange(f"p ({names}) -> p {names}", **kw)

    def alloc(self, shape, dtype):
        off = (self.lo + 63) // 64 * 64
        n = int(np.prod(shape)) * ISZ[dtype]
        assert off + n <= self.hi, f"arena overflow: need {off + n} have {self.hi}"
        self.lo = off + n
        return self._view(off, shape, dtype)

    def alloc_top(self, shape, dtype):
        n = int(np.prod(shape)) * ISZ[dtype]
        off = (self.hi - n) // 64 * 64
        assert off >= self.lo, "arena overflow (top)"
        self.hi = off
        return self._view(off, shape, dtype)


class Rot:
    def __init__(self, aps, name):
        self.aps = aps
        self.name = name
        self.i = -1

    def next(self):
        self.i += 1
        k = self.i % len(self.aps)
        return self.aps[k], f"{self.name}{k}"


class B:
    def __init__(self, cfg):
        self.cfg = cfg
        nc = self.nc = bass.Bass("TRN2", target_bir_lowering=False)
        self.P = Prog(nc)
        self.dr = {}
        self.outs = []
        self.A = Arena(nc, 207 * 1024)
        self.ps = [nc.alloc_psum_tensor(f"psb{i}", [128, 512], F32) for i in range(8)]
        self.psrot = 0

    def din(self, name, shape, dt=F32):
        self.dr[name] = self.nc.dram_tensor(name, list(shape), dt, kind="ExternalInput").ap()
        return self.dr[name]

    def dout(self, name, shape, dt=F32):
        self.dr[name] = self.nc.dram_tensor(name, list(shape), dt, kind="ExternalOutput").ap()
        return self.dr[name]

    def dscr(self, name, shape, dt):
        kind = "ExternalOutput" if name in self.cfg.get("debug", ()) else "Internal"
        self.dr[name] = self.nc.dram_tensor(name, list(shape), dt, kind=kind).ap()
        return self.dr[name]

    def mm(self, out, lhsT, rhs, start, stop, r, w, nocheck=False):
        if nocheck:
            return self.P.op("pe", lambda h: h.matmul(out, lhsT=lhsT, rhs=rhs, start=start, stop=stop, skip_group_check=True), r, w)
        return self.P.op("pe", lambda h: h.matmul(out, lhsT=lhsT, rhs=rhs, start=start, stop=stop), r, w)

    def tp(self, out, in_, ident, r, w):
        return self.P.op("pe", lambda h: h.transpose(out=out, in_=in_, identity=ident), r, w)

    def act(self, out, in_, func, r, w, **kw):
        return self.P.op("act", lambda h: h.activation(out=out, in_=in_, func=func, **kw), r, w)

    def tt(self, eng, out, in0, in1, op, r, w):
        return self.P.op(eng, lambda h: h.tensor_tensor(out=out, in0=in0, in1=in1, op=op), r, w)

    def ts(self, eng, out, in0, s1, s2, op0, op1, r, w):
        if op1 is None:
            return self.P.op(eng, lambda h: h.tensor_scalar(out=out, in0=in0, scalar1=s1, scalar2=None, op0=op0), r, w)
        return self.P.op(eng, lambda h: h.tensor_scalar(out=out, in0=in0, scalar1=s1, scalar2=s2, op0=op0, op1=op1), r, w)

    def stt(self, out, in0, scalar, in1, op0, op1, r, w):
        return self.P.op("dve", lambda h: h.scalar_tensor_tensor(out=out, in0=in0, scalar=scalar, in1=in1, op0=op0, op1=op1), r, w)

    def cp(self, eng, out, in_, r, w):
        if eng == "act":
            return self.P.op("act", lambda h: h.copy(out=out, in_=in_), r, w)
        return self.P.op(eng, lambda h: h.tensor_copy(out=out, in_=in_), r, w)

    def dma(self, q, out, in_, r, w, semkey, cast=False):
        r = [x for x in r if x not in self.dr]
        w = [x for x in w if x not in self.dr]
        if cast:
            return self.P.op("pool", lambda h: h.dma_start(out=out, in_=in_, max_dma_last_dim=4096), r, w, dma=True, semkey=semkey)
        return self.P.op(q, lambda h: h.dma_start(out=out, in_=in_), r, w, dma=True, semkey=semkey)

    def rsq(self, out, in_, inv_n, r, w):
        self.act(out, in_, AF.Sqrt, r + ["epsc"], w, scale=inv_n, bias=self.c["epsc"][0:out.shape[0], 0:1])
        self.P.op("dve", lambda h: h.reciprocal(out=out, in_=out), w, w)

    def bank(self):
        self.psrot = (self.psrot + 1) % 8
        return self.psrot


def ext_cols(j):
    if j < 8:
        return 144 * j + 16, 144 * j + 144
    return (1152, 1280)


def phase_consts(k):
    A, P = k.A, k.P
    c = k.c = {}
    c["identf"] = A.alloc([128], F32)
    c["identb"] = A.alloc([128], BF16)
    c["onesb"] = A.alloc([128], BF16)
    c["g1c"] = A.alloc([32], F32)
    c["g2c"] = A.alloc([32], F32)
    c["gqc"] = A.alloc([8], F32)
    c["gkvc"] = A.alloc([4], F32)
    c["psc"] = A.alloc([16], F32)
    c["ckvTs"] = A.alloc([4, 128], BF16)
    c["kpeTs"] = A.alloc([128], BF16)
    c["r2c"] = A.alloc([9], F32)
    c["r3c"] = A.alloc([9], F32)
    c["epsc"] = A.alloc([1], F32)
    P.op("pool", lambda h: h.memset(c["epsc"][:, :], EPS), (), ["epsc"])
    idf = c["identf"]
    P.op("pool", lambda h: h.memset(idf[:, :], 1.0), (), ["identf"])
    P.op("pool", lambda h: h.affine_select(out=idf[:, :], in_=idf[:, :], pattern=[[-1, 128]], compare_op=ALU.is_equal,
                                           fill=0.0, base=0, channel_multiplier=1), ["identf"], ["identf"])
    k.cp("dve", c["identb"][:, :], idf[:, :], ["identf"], ["identb"])
    P.op("pool", lambda h: h.memset(c["onesb"][:, :], 1.0), (), ["onesb"])
    for nm in ("g1c", "g2c", "gqc", "gkvc", "psc"):
        k.dma("sp", c[nm][:, :], k.dr[nm][:, :], (), [nm], semkey="c_" + nm)


def load_w_block(k, dst, src, M, nch, gcol, stage, skey, r, w):
    k.dma("sp", stage[:, 0:nch, 0:M], src, (), [skey], semkey=skey)
    k.tt("dve", dst, stage[:, 0:nch, 0:M], gcol.unsqueeze(2).to_broadcast([128, nch, M]), ALU.mult, [skey] + r, w)


def phase_A(k):
    A, P, c, ps = k.A, k.P, k.c, k.ps
    lo0, hi0 = A.lo, A.hi
    ng = k.cfg.get("nga", 16)
    wkv = A.alloc([32, 640], BF16)
    lo_w = (A.lo + 63) // 64 * 64
    wst = A.alloc([32, 128], F32)
    A.lo = lo_w
    sqall = A.alloc([32, 512], BF16)
    xa = Rot([A.alloc([32, 512], BF16) for _ in range(2)], "xa")
    r1 = A.alloc([512], F32)
    zk = A.alloc([4, 512], F32)
    sq2 = Rot([A.alloc([512], BF16) for _ in range(2)], "sq2a")
    rkv = A.alloc([512], F32)
    co = Rot([A.alloc([4, 512], BF16) for _ in range(2)], "coa")
    cs = Rot([A.alloc([2, 512], F32) for _ in range(2)], "csa")
    t1 = A.alloc([512], F32)
    t2 = A.alloc([512], F32)
    ko = Rot([A.alloc([512], BF16) for _ in range(2)], "koa")
    wblk, wpe = k.dr["w_in_blk"], k.dr["w_pe_blk"]
    for i in range(4):
        load_w_block(k, wkv[:, :, i * 128:(i + 1) * 128], wblk[8 + i], 128, 32, c["g1c"], wst, "wst", ["g1c"], [f"wkv{i}"])
    for i in range(2):
        load_w_block(k, wkv[:, :, 512 + 64 * i:576 + 64 * i], wpe[i], 64, 32, c["g1c"], wst, "wst", ["g1c"], [f"wkv{4 + i}"])
    P.op("act", lambda h: h.mul(out=wkv[:, :, 576:608], in_=wkv[:, :, 576:608], mul=-1.0), ["wkv5"], ["wkv5"])
    WK = [f"wkv{i}" for i in range(6)]
    zr = Rot([A.alloc([6, 512], F32) for _ in range(2)], "zra")
    xs, xks = [], []

    def load(g):
        x, xk = xa.next()
        k.dma("pool", x, k.dr["xTa"][g], (), [xk], semkey=xk, cast=True)
        xs.append(x)
        xks.append(xk)

    load(0)
    for g in range(ng):
        x, xk = xs[g], xks[g]
        if g + 1 < ng:
            load(g + 1)
        for q4 in range(4):
            k.act(sqall[:, q4 * 8:(q4 + 1) * 8, :], x[:, q4 * 8:(q4 + 1) * 8, :], AF.Square, [xk], [f"sqall{q4}"] + (["wst"] if g == 0 else []))
        z, zk_ = zr.next()
        for eb in range(4):
            for ch in range(32):
                k.mm(ps[1 + eb][:, :], wkv[:, ch, eb * 128:(eb + 1) * 128], x[:, ch, :], ch == 0, ch == 31, [xk] + WK, [f"psA{1 + eb}"])
            k.cp("act" if eb % 2 == 0 else "dve", z[:, eb, :], ps[1 + eb][:, :], [f"psA{1 + eb}"], [f"{zk_}_{eb}"])
        for i in range(2):
            for ch in range(32):
                k.mm(ps[5 + i][0:64, :], wkv[:, ch, 512 + 64 * i:576 + 64 * i], x[:, ch, :], ch == 0, ch == 31, [xk] + WK, [f"psA{5 + i}"])
            k.cp("dve" if i == 0 else "act", z[0:64, 4 + i, :], ps[5 + i][0:64, :], [f"psA{5 + i}"], [f"{zk_}_{4 + i}"])
        for ch in range(32):
            k.mm(ps[0][:, :], c["onesb"][:, :], sqall[:, ch, :], ch == 0, ch == 31, [f"sqall{ch // 8}", "onesb"], ["psA0"])
        k.rsq(r1[:, :], ps[0][:, :], 1.0 / D, ["psA0"], ["r1a"])
        for eb in range(4):
            k.tt("dve" if eb % 2 == 0 else "pool", zk[:, eb, :], z[:, eb, :], r1[:, :], ALU.mult, [f"{zk_}_{eb}", "r1a"], [f"zk{eb}"])
            s, sk = sq2.next()
            k.act(s[:, :], zk[:, eb, :], AF.Square, [f"zk{eb}"], [sk])
            k.mm(ps[7][:, :], c["onesb"][:, :], s[:, :], eb == 0, eb == 3, [sk, "onesb"], ["psA7"])
        k.rsq(rkv[:, :], ps[7][:, :], 1.0 / 512, ["psA7"], ["rkva"])
        o, ok_ = co.next()
        for eb in range(4):
            k.stt(o[:, eb, :], zk[:, eb, :], c["gkvc"][:, eb:eb + 1], rkv[:, :], ALU.mult, ALU.mult,
                  [f"zk{eb}", "rkva", "gkvc"], [ok_])
        k.dma("sp", k.dr["ckvT_s"][:, :, g * 512:(g + 1) * 512], o, [ok_], [], semkey=ok_)
        t, tk = cs.next()
        k.dma("sp", t[0:64, :, :], k.dr["csTa"][:, :, g * 512:(g + 1) * 512], (), [tk], semkey=tk)
        k.tt("dve", t1[0:64, :], z[0:64, 4, :], t[0:64, 0, :], ALU.mult, [f"{zk_}_4", tk], ["t1a"])
        k.tt("pool", t2[0:64, :], z[0:64, 5, :], t[0:64, 1, :], ALU.mult, [f"{zk_}_5", tk], ["t2a"])
        k.tt("pool", t1[0:64, :], t1[0:64, :], t2[0:64, :], ALU.add, ["t1a", "t2a"], ["t1a"])
        o2, ok2 = ko.next()
        k.tt("dve", o2[0:64, :], t1[0:64, :], r1[0:64, :], ALU.mult, ["t1a", "r1a"], [ok2])
        k.dma("sp", k.dr["kpeT_s"][:, g * 512:(g + 1) * 512], o2[0:64, :], [ok2], [], semkey=ok2)
    P.barrier()
    A.lo, A.hi = lo0, hi0


def phase_B(k):
    A, P, c, ps = k.A, k.P, k.c, k.ps
    lo0, hi0 = A.lo, A.hi
    wblk, wpe = k.dr["w_in_blk"], k.dr["w_pe_blk"]
    ones = c["onesb"]
    xT = A.alloc([32, TE], BF16)
    r1 = A.alloc([TE], F32)
    wst = A.alloc([32, 128], F32)
    wb = Rot([A.alloc([32, 128], BF16) for _ in range(2)], "wbB")
    sq = Rot([A.alloc([512], BF16) for _ in range(3)], "sqB")
    nqT = A.alloc_top([8, TE], BF16)
    lo1 = A.lo
    for q4 in range(4):
        k.dma("pool", xT[:, q4 * 8:(q4 + 1) * 8, :], k.dr["xTo"][:, q4 * 8:(q4 + 1) * 8, :], (), [f"xT{q4}"], semkey=f"xT{q4}", cast=True)
    cnt = [0]

    def ssq_bc(srcs, n_src, out, inv_n, okey):
        for pi, (a, b_) in enumerate(PIECES_TE):
            cnt[0] += 1
            bk = 6 + (cnt[0] % 2)
            for i in range(n_src):
                ap, keys = srcs(i, a, b_)
                s, sk = sq.next()
                k.act(s[:, 0:b_ - a], ap, AF.Square, keys, [sk])
                k.mm(ps[bk][:, 0:b_ - a], ones[:, :], s[:, 0:b_ - a], i == 0, i == n_src - 1, [sk, "onesb"], [f"ps{bk}"])
            k.rsq(out[:, a:b_], ps[bk][:, 0:b_ - a], inv_n, [f"ps{bk}"], [f"{okey}{pi}"])

    ssq_bc(lambda ch, a, b_: (xT[:, ch, a:b_], [f"xT{ch // 8}"]), 32, r1, 1.0 / D, "r1_")
    R1 = ["r1_0", "r1_1", "r1_2"]
    bset = [0]

    ucast = [0]

    def zblock(src, M, post, neg=None):
        w, wk = wb.next()
        load_w_block(k, w[:, :, 0:M], src, M, 32, c["g1c"], wst, "wstB", ["g1c"], [wk])
        if neg is not None:
            P.op("act", lambda h: h.mul(out=w[:, :, neg[0]:neg[1]], in_=w[:, :, neg[0]:neg[1]], mul=-1.0), [wk], [wk])
        bset[0] ^= 1
        base = 3 * bset[0]
        for pi, (a, b_) in enumerate(PIECES_TE):
            pk = f"ps{base + pi}"
            for ch in range(32):
                k.mm(ps[base + pi][0:M, 0:b_ - a], w[:, ch, 0:M], xT[:, ch, a:b_], ch == 0, ch == 31, [wk, f"xT{ch // 8}"], [pk])
            post(pi, a, b_, ps[base + pi][0:M, 0:b_ - a], pk)

    if k.cfg.get("B1", True):
        for eb in range(8):
            def post(pi, a, b_, p, pk, eb=eb):
                k.tt("dve", nqT[:, eb, a:b_], p, r1[:, a:b_], ALU.mult, [pk, f"r1_{pi}"], [f"nq{eb}_{pi}"])
            zblock(wblk[eb], 128, post)
        rq = A.alloc([TE], F32)
        ssq_bc(lambda eb, a, b_: (nqT[:, eb, a:b_], [f"nq{eb}_{PIECES_TE.index((a, b_))}"]), 8, rq, 1.0 / 1024, "rq_")
        for eb in range(8):
            for pi, (a, b_) in enumerate(PIECES_TE):
                k.tt("dve", nqT[:, eb, a:b_], nqT[:, eb, a:b_], rq[:, a:b_], ALU.mult, [f"nq{eb}_{pi}", f"rq_{pi}"], [f"nq{eb}_{pi}"])
    A.lo = lo1
    zk = A.alloc([4, TE], F32)
    rkv = A.alloc([TE], F32)
    pe = A.alloc([2, TE], F32)
    cs = A.alloc([2, TE], F32)
    ost = Rot([A.alloc([576], F32) for _ in range(2)], "ostB")
    for eb in range(4):
        def post(pi, a, b_, p, pk, eb=eb):
            k.tt("dve", zk[:, eb, a:b_], p, r1[:, a:b_], ALU.mult, [pk, f"r1_{pi}"], [f"zkB{eb}_{pi}"])
        zblock(wblk[8 + eb], 128, post)
    ssq_bc(lambda eb, a, b_: (zk[:, eb, a:b_], [f"zkB{eb}_{PIECES_TE.index((a, b_))}"]), 4, rkv, 1.0 / 512, "rkvB")
    for eb in range(4):
        for pi, (a, b_) in enumerate(PIECES_TE):
            k.stt(zk[:, eb, a:b_], zk[:, eb, a:b_], c["gkvc"][:, eb:eb + 1], rkv[:, a:b_], ALU.mult, ALU.mult,
                  [f"zkB{eb}_{pi}", f"rkvB{pi}", "gkvc"], [f"zkB{eb}_{pi}"])
        k.cp("pool", c["ckvTs"][:, eb, :], zk[:, eb, 1152:1280], [f"zkB{eb}_2"], [f"ckvTs{eb}"])
    k.dma("sp", cs[0:64, :, :], k.dr["csTo"][:, :, :], (), ["csB"], semkey="csB")
    for i in range(2):
        def post(pi, a, b_, p, pk, i=i):
            k.tt("dve", pe[0:64, i, a:b_], p, cs[0:64, i, a:b_], ALU.mult, [pk, "csB"], [f"peB{i}_{pi}"])
        zblock(wpe[i], 64, post, neg=(0, 32) if i == 1 else None)
    for pi, (a, b_) in enumerate(PIECES_TE):
        k.tt("pool", pe[0:64, 0, a:b_], pe[0:64, 0, a:b_], pe[0:64, 1, a:b_], ALU.add, [f"peB0_{pi}", f"peB1_{pi}"], [f"peB0_{pi}"])
        k.tt("dve", pe[0:64, 0, a:b_], pe[0:64, 0, a:b_], r1[0:64, a:b_], ALU.mult, [f"peB0_{pi}", f"r1_{pi}"], [f"peB0_{pi}"])
    k.cp("pool", c["kpeTs"][0:64, :], pe[0:64, 0, 1152:1280], ["peB0_2"], ["kpeTs"])
    ZK = [f"zkB{eb}_{pi}" for eb in range(4) for pi in range(3)]
    PE0 = [f"peB0_{pi}" for pi in range(3)]
    for j in range(9):
        a, b_ = ext_cols(j)
        o, ok_ = ost.next()
        bk = 6 + (j % 2)
        for eb in range(4):
            k.tp(ps[bk][:, eb * 128:(eb + 1) * 128], zk[:, eb, a:b_], c["identf"][:, :], ZK + ["identf"], [f"ps{bk}"])
        k.cp("act", o[:, 0:512], ps[bk][:, :], [f"ps{bk}"], [ok_ + "a"])
        k.dma("sp", k.dr["ckv_own"][j], o[:, 0:512], [ok_ + "a"], ["ckv_own"], semkey=ok_ + "a")
        bk2 = 6 + ((j + 1) % 2)
        k.tp(ps[bk2][:, 0:64], pe[0:64, 0, a:b_], c["identf"][0:64, 0:64], PE0 + ["identf"], [f"ps{bk2}"])
        k.cp("dve", o[:, 512:576], ps[bk2][:, 0:64], [f"ps{bk2}"], [ok_ + "b"])
        k.dma("sp", k.dr["kpe_own"][j], o[:, 512:576], [ok_ + "b"], ["kpe_own"], semkey=ok_ + "b")
    A.lo = lo1
    if k.cfg.get("B3", True):
        dT = A.alloc([16, TC], BF16)
        lo_d = A.lo
        u = Rot([A.alloc([TU], F32) for _ in range(2)], "uB")
        sb = [A.alloc([TU], F32) for _ in range(2)]
        inv0 = A.alloc([4, 144], F32)
        pst = Rot([A.alloc([3, 128], F32) for _ in range(2)], "pstB")
        tmp0 = A.alloc([128], F32)
        k.dma("sp", inv0, k.dr["inv0"], (), ["inv0"], semkey="inv0")
        for blk in range(16):
            g = blk // 4
            w_ = 2 << g
            uu, uk = u.next()

            def post(pi, a, b_, p, pk, uu=uu, uk=uk):
                if pi < 2:
                    k.tt("dve", uu[:, a:b_], p, r1[:, a:b_], ALU.mult, [pk, f"r1_{pi}"], [f"{uk}_{pi}"])
                else:
                    k.tt("dve", uu[:, 1024:1152], p[:, 0:128], r1[:, 1024:1152], ALU.mult, [pk, "r1_2"], [f"{uk}_2"])
                    k.tt("dve", uu[:, 1168:1232], p[:, 128:192], r1[:, 1152:1216], ALU.mult, [pk, "r1_2"], [f"{uk}_3"])
                    k.tt("dve", uu[:, 1248:1312], p[:, 192:256], r1[:, 1216:1280], ALU.mult, [pk, "r1_2"], [f"{uk}_4"])
            zblock(wblk[12 + blk], 128, post)
            k.dma("sp", uu[:, 1153:1168], k.dr["spT"][0][:, blk, :], (), [f"{uk}_5"], semkey=f"{uk}_5")
            k.dma("sp", uu[:, 1233:1248], k.dr["spT"][1][:, blk, :], (), [f"{uk}_6"], semkey=f"{uk}_6")
            UK = [f"{uk}_{i}" for i in range(7)]
            cur, ck = uu, UK
            for st in range(g + 1):
                sh = 1 << st
                nxt = sb[st % 2]
                k.tt("pool" if st % 2 == 0 else "dve", nxt[:, sh:TU], cur[:, sh:TU], cur[:, 0:TU - sh], ALU.add, ck, [f"sB{st % 2}"])
                cur, ck = nxt, [f"sB{st % 2}"]
            S = cur
            k.stt(dT[:, blk, 0:1024].rearrange("p (j t) -> p j t", t=128), S[:, 0:1152].rearrange("p (j t) -> p j t", t=144)[:, :, 16:144],
                  1.0 / w_, uu[:, 0:1152].rearrange("p (j t) -> p j t", t=144)[:, :, 16:144], ALU.mult, ALU.subtract, ck + UK, [f"dT{blk}"])
            k.stt(dT[:, blk, 1024:1088], S[:, 1168:1232], 1.0 / w_, uu[:, 1168:1232], ALU.mult, ALU.subtract, ck + UK, [f"dT{blk}"])
            k.stt(dT[:, blk, 1088:1152], S[:, 1248:1312], 1.0 / w_, uu[:, 1248:1312], ALU.mult, ALU.subtract, ck + UK, [f"dT{blk}"])
            k.tt("dve", tmp0[:, :], S[:, 16:144], inv0[:, g, 16:144], ALU.mult, ck + ["inv0"], ["tmp0"])
            k.tt("dve", dT[:, blk, 0:128], tmp0[:, :], uu[:, 16:144], ALU.subtract, ["tmp0"] + UK, [f"dT{blk}"])
            o, ok_ = pst.next()
            for i, c0 in enumerate((1137, 1217, 1297)):
                k.tp(ps[7][0:15, i * 128:(i + 1) * 128], uu[:, c0:c0 + 15], c["identf"][:, :], UK + ["identf"], ["ps7"])
            k.cp("act", o[0:15, :, :], ps[7][0:15, 0:384].rearrange("p (a b) -> p a b", b=128), ["ps7"], [ok_])
            k.dma("sp", k.dr["pool_own"][:, :, blk * 128:(blk + 1) * 128].rearrange("a r c -> r a c"), o[0:15, :, :], [ok_], ["pool_own"], semkey=ok_)
        P.barrier()
        A.lo = lo_d
        pw = A.alloc([64, 128], BF16)
        osb = Rot([A.alloc([TC], BF16) for _ in range(2)], "osbB")
        k.dma("pool", pw, k.dr["pool_w_blk"], (), ["pw"], semkey="pw", cast=True)
        for g in range(4):
            for eb in range(4):
                bset[0] ^= 1
                base = 3 * bset[0]
                o, ok_ = osb.next()
                for pi, (a, b_) in enumerate(PIECES_TC):
                    pk = f"ps{base + pi}"
                    for cc in range(4):
                        k.mm(ps[base + pi][:, 0:b_ - a], pw[:, (g * 4 + eb) * 4 + cc, :], dT[:, g * 4 + cc, a:b_], cc == 0, cc == 3, ["pw"], [pk])
                    k.act(o[:, a:b_], ps[base + pi][:, 0:b_ - a], AF.Copy, [pk, "psc"], [ok_], scale=c["psc"][:, g * 4 + eb:g * 4 + eb + 1])
                k.dma("sp", k.dr["oT_s"][16 + g * 4 + eb], o[:, :], [ok_], ["oT_s"], semkey=ok_)
    P.barrier()
    A.lo = lo0
    if k.cfg.get("B1", True):
        wqs = A.alloc([8, 256], F32)
        wqb = Rot([A.alloc([8, 256], BF16) for _ in range(2)], "wqbB")
        cs = A.alloc([2, TE], F32)
        qo = Rot([A.alloc([TE], BF16) for _ in range(2)], "qoB")
        qpo = Rot([A.alloc([TE], BF16) for _ in range(2)], "qpoB")
        t1 = A.alloc([TE], F32)
        t2 = A.alloc([TE], F32)
        k.dma("sp", cs[0:64, :, :], k.dr["csTo"][:, :, :], (), ["csB4"], semkey="csB4")
        NQ = [f"nq{eb}" for eb in range(8)]
        for h in range(16):
            w, wk = wqb.next()
            load_w_block(k, w[:, :, :], k.dr["w_uq_blk"][h], 256, 8, c["gqc"], wqs, "wqsB", ["gqc"], [wk])
            P.op("act", lambda hh, w=w: hh.mul(out=w[:, :, 192:224], in_=w[:, :, 192:224], mul=-1.0), [wk], [wk])
            o, ok_ = qo.next()
            bset[0] ^= 1
            base = 3 * bset[0]
            for pi, (a, b_) in enumerate(PIECES_TE):
                pk = f"ps{base + pi}"
                for ch in range(8):
                    k.mm(ps[base + pi][:, 0:b_ - a], w[:, ch, 0:128], nqT[:, ch, a:b_], ch == 0, ch == 7, [wk] + NQ, [pk])
                k.act(o[:, a:b_], ps[base + pi][:, 0:b_ - a], AF.Copy, [pk], [ok_], scale=SCALE)
            k.dma("sp", k.dr["qT_s"][h][:, 0:1024].rearrange("p (j t) -> p j t", t=128),
                  o[:, 0:1152].rearrange("p (j t) -> p j t", t=144)[:, :, 16:144], [ok_], [], semkey=ok_)
            k.dma("sp", k.dr["qT_s"][h][:, 1024:1152], o[:, 1152:1280], [ok_], [], semkey=ok_)
            for i in range(2):
                bset[0] ^= 1
                base = 3 * bset[0]
                tt_ = t1 if i == 0 else t2
                for pi, (a, b_) in enumerate(PIECES_TE):
                    pk = f"ps{base + pi}"
                    for ch in range(8):
                        k.mm(ps[base + pi][0:64, 0:b_ - a], w[:, ch, 128 + 64 * i:192 + 64 * i], nqT[:, ch, a:b_], ch == 0, ch == 7, [wk] + NQ, [pk])
                    k.tt("dve", tt_[0:64, a:b_], ps[base + pi][0:64, 0:b_ - a], cs[0:64, i, a:b_], ALU.mult, [pk, "csB4"], [f"tq{i}"])
            k.tt("pool", t1[0:64, :], t1[0:64, :], t2[0:64, :], ALU.add, ["tq0", "tq1"], ["tq0"])
            o2, ok2 = qpo.next()
            k.act(o2[0:64, :], t1[0:64, :], AF.Copy, ["tq0"], [ok2], scale=SCALE)
            k.dma("sp", k.dr["qpT_s"][h][:, 0:1024].rearrange("p (j t) -> p j t", t=128),
                  o2[0:64, 0:1152].rearrange("p (j t) -> p j t", t=144)[:, :, 16:144], [ok2], [], semkey=ok2)
            k.dma("sp", k.dr["qpT_s"][h][:, 1024:1152], o2[0:64, 1152:1280], [ok2], [], semkey=ok2)
    P.barrier()
    A.lo, A.hi = lo0, hi0


def phase_attn(k):
    A, P, c, ps = k.A, k.P, k.c, k.ps
    lo0, hi0 = A.lo, A.hi
    NH = k.cfg.get("nheads", 16)
    ckv = A.alloc([4, 8192], BF16)
    kpe = A.alloc([8192], BF16)
    mk = A.alloc([8, 128], BF16)
    cks = A.alloc([2, 4, 1088], BF16)
    kps = A.alloc([2, 1088], BF16)
    KT = A.alloc([8192], BF16)
    V = A.alloc([64, 128], BF16)
    KTs = A.alloc([2, 1088], BF16)
    Vs = A.alloc([2, 9, 136], BF16)
    wkv = Rot([A.alloc([4, 256], BF16) for _ in range(2)], "wkvT")
    qT = Rot([A.alloc([TC], BF16) for _ in range(2)], "qTt")
    qpT = Rot([A.alloc([TC], BF16) for _ in range(2)], "qpTt")
    PT = Rot([A.alloc([512], BF16) for _ in range(6)], "PTt")
    ob = Rot([A.alloc([128], BF16) for _ in range(2)], "obt")
    rinv = Rot([A.alloc([1], F32) for _ in range(2)], "rinvt")
    rec = A.alloc([1024], F32)
    oT = Rot([A.alloc([TC], BF16) for _ in range(2)], "oTt")
    for ch in range(4):
        k.dma("sp", ckv[:, ch, :], k.dr["ckvT_s"][:, ch, :], (), [f"ckv{ch}"], semkey=f"ckv{ch}")
    CKV = [f"ckv{ch}" for ch in range(4)]
    k.dma("sp", kpe[0:64, :], k.dr["kpeT_s"][:, :], (), ["kpe"], semkey="kpe")
    k.dma("pool", mk, k.dr["maskT"], (), ["mk"], semkey="mk", cast=True)
    for b in range(2):
        k.dma("pool", cks[:, b, :, 0:1024], k.dr["cckvT"][b], (), [f"cks{b}"], semkey=f"cks{b}", cast=True)
        k.dma("pool", kps[0:64, b, 0:1024], k.dr["ckpeT"][b], (), [f"kps{b}"], semkey=f"kps{b}", cast=True)
        k.cp("dve", cks[:, b, :, 1024:1088], c["ckvTs"][:, :, 64 * b:64 * b + 64], [], [f"cksn{b}"])
        k.cp("dve", kps[0:64, b, 1024:1088], c["kpeTs"][0:64, 64 * b:64 * b + 64], [], [f"kpsn{b}"])
    P.op("pool", lambda h: h.memset(Vs[:, :, :, 128:129], 1.0), (), ["Vsones"])
    rk = [0]
    rs = [0]

    def bkv():
        rk[0] = (rk[0] + 1) % 2
        return rk[0]

    def bs_():
        rs[0] = (rs[0] + 1) % 2
        return 2 + rs[0]

    for h in range(NH):
        wk, wkk = wkv.next()
        q, qk = qT.next()
        qp, qpk = qpT.next()
        k.dma("pool", wk, k.dr["w_ukv_blk"][h], (), [wkk], semkey=wkk, cast=True)
        k.dma("sp", q[:, :], k.dr["qT_s"][h], (), [qk], semkey=qk)
        k.dma("sp", qp[0:64, :], k.dr["qpT_s"][h], (), [qpk], semkey=qpk)
        o_t, otk = oT.next()
        vv_ = k.dr["peer_v"].rearrange("(c p) d -> p c d", p=128)
        for i8 in range(8):
            ds_, cg_ = divmod(h * 8 + i8, 16)
            k.dma("pool", k.dr["u16_s"][h * 8 + i8], k.dr["uT_blk"][h * 8 + i8], (), [], semkey="u16", cast=True)
        for pc in range(16):
            b = bkv()
            for ch in range(4):
                k.mm(ps[b][:, :], wk[:, ch, 0:128], ckv[:, ch, pc * 512:(pc + 1) * 512], ch == 0, ch == 3, [wkk] + CKV, [f"ps{b}"])
            k.cp("act", KT[:, pc * 512:(pc + 1) * 512], ps[b][:, :], [f"ps{b}"], [f"KT{pc}"])
        for vg in range(16):
            b = bkv()
            for i in range(4):
                sb = vg * 4 + i
                for ch in range(4):
                    k.mm(ps[b][:, i * 128:(i + 1) * 128], ckv[:, ch, sb * 128:(sb + 1) * 128], wk[:, ch, 128:256], ch == 0, ch == 3, [wkk] + CKV, [f"ps{b}"])
            k.cp("dve", V[:, vg * 4:vg * 4 + 4, :], ps[b][:, :].rearrange("p (a e) -> p a e", e=128), [f"ps{b}"], [f"V{vg}"])
        for bb in range(2):
            SK = [f"cks{bb}", f"cksn{bb}"]
            for (a, b_) in ((0, 512), (512, 1024), (1024, 1088)):
                b = bkv()
                for ch in range(4):
                    k.mm(ps[b][:, 0:b_ - a], wk[:, ch, 0:128], cks[:, bb, ch, a:b_], ch == 0, ch == 3, [wkk] + SK, [f"ps{b}"])
                k.cp("act", KTs[:, bb, a:b_], ps[b][:, 0:b_ - a], [f"ps{b}"], [f"KTs{bb}"])
            for vg in range(3):
                b = bkv()
                blks = range(vg * 4, min(vg * 4 + 4, 9))
                for i, sb in enumerate(blks):
                    rows = 128 if sb < 8 else 64
                    for ch in range(4):
                        k.mm(ps[b][0:rows, i * 128:(i + 1) * 128], cks[:, bb, ch, sb * 128:sb * 128 + rows], wk[:, ch, 128:256], ch == 0, ch == 3,
                             [wkk] + SK, [f"ps{b}"])
                if vg < 2:
                    k.cp("dve", Vs[:, bb, vg * 4:vg * 4 + 4, 0:128], ps[b][:, :].rearrange("p (a e) -> p a e", e=128), [f"ps{b}"], [f"Vs{bb}"])
                else:
                    k.cp("dve", Vs[0:64, bb, 8, 0:128], ps[b][0:64, 0:128], [f"ps{b}"], [f"Vs{bb}"])
        steps = []
        for sb in range(64):
            g = sb // 8
            for (a, b_) in ([(128 * g, 512), (512, 1024)] if g < 4 else [(128 * g, 1024)]):
                steps.append((sb, a, b_))
        SKEW = 3

        def scores(i):
            sb, a, b_ = steps[i]
            n = b_ - a
            bs = i % 4
            k.mm(ps[bs][:, 0:n], KT[:, sb * 128:(sb + 1) * 128], q[:, a:b_], True, False, [f"KT{sb // 4}", qk], [f"ps{bs}"])
            k.mm(ps[bs][:, 0:n], kpe[0:64, sb * 128:(sb + 1) * 128], qp[0:64, a:b_], False, True, ["kpe", qpk], [f"ps{bs}"])

        for i in range(min(SKEW, len(steps))):
            scores(i)
        for i, (sb, a, b_) in enumerate(steps):
            if i + SKEW < len(steps):
                scores(i + SKEW)
            g = sb // 8
            n = b_ - a
            bs = i % 4
            p, pk = PT.next()
            k.act(p[:, 0:n], ps[bs][:, 0:n], AF.Exp, [f"ps{bs}"], [pk])
            if a == 128 * g:
                k.tt("pool", p[:, 0:128], p[:, 0:128], mk[:, sb % 8, :], ALU.mult, [pk, "mk"], [pk])
            ob_, oc = (4, a) if a < 512 else (5, a - 512)
            k.mm(ps[ob_][:, oc:oc + n], V[:, sb, :], p[:, 0:n], sb == 0, sb == 63, [pk, f"V{sb // 4}"], [f"ps{ob_}"], nocheck=True)
            k.mm(ps[ob_ + 2][:, oc:oc + n], c["onesb"][:, :], p[:, 0:n], sb == 0, sb == 63, [pk, "onesb"], [f"ps{ob_ + 2}"], nocheck=True)
        for i in range(2):
            P.op("dve", lambda hh, i=i: hh.reciprocal(out=rec[:, i * 512:(i + 1) * 512], in_=ps[6 + i][:, :]), [f"ps{6 + i}"], [f"rec{i}"])
            k.tt("dve", o_t[:, i * 512:(i + 1) * 512], ps[4 + i][:, :], rec[:, i * 512:(i + 1) * 512], ALU.mult, [f"ps{4 + i}", f"rec{i}"], [otk])

        def finish(bo, rows, oc0, pk_):
            rv, rvk = rinv.next()
            P.op("dve", lambda hh: hh.reciprocal(out=rv[0:rows, :], in_=ps[bo][0:rows, 128:129]), [pk_], [rvk])
            o, okk = ob.next()
            k.act(o[0:rows, :], ps[bo][0:rows, 0:128], AF.Copy, [pk_, rvk], [okk], scale=rv[0:rows, 0:1])
            b = bkv()
            pv = ps[b][:, 0:64].bitcast(BF16)
            k.tp(pv[:, 0:rows], o[0:rows, :], c["identb"][0:rows, 0:rows], [okk, "identb"], [f"ps{b}"])
            k.cp("dve", o_t[:, oc0:oc0 + rows], pv[:, 0:rows], [f"ps{b}"], [otk])

        for bb in range(2):
            qa = 1024 + 64 * bb
            bo = bkv()
            for vg in range(3):
                bs = bs_()
                blks = list(range(vg * 4, min(vg * 4 + 4, 9)))
                rows = 128 if vg < 2 else 64
                for i, sb in enumerate(blks):
                    k.mm(ps[bs][0:rows, i * 64:(i + 1) * 64], KTs[:, bb, sb * 128:sb * 128 + rows], q[:, qa:qa + 64], True, False, [f"KTs{bb}", qk], [f"ps{bs}"])
                    k.mm(ps[bs][0:rows, i * 64:(i + 1) * 64], kps[0:64, bb, sb * 128:sb * 128 + rows], qp[0:64, qa:qa + 64], False, True,
                         [f"kps{bb}", f"kpsn{bb}", qpk], [f"ps{bs}"])
                p, pk = PT.next()
                n = len(blks) * 64
                k.act(p[0:rows, 0:n], ps[bs][0:rows, 0:n], AF.Exp, [f"ps{bs}"], [pk])
                for i, sb in enumerate(blks):
                    k.mm(ps[bo][0:64, 0:129], p[0:rows, i * 64:(i + 1) * 64], Vs[0:rows, bb, sb, 0:129], sb == 0, sb == 8, [pk, f"Vs{bb}", "Vsones"], [f"ps{bo}"])
            finish(bo, 64, 1024 + 64 * bb, f"ps{bo}")
        k.dma("sp", k.dr["oT_s"][h], o_t[:, :], [otk], [], semkey=otk)
    P.barrier()
    A.lo, A.hi = lo0, hi0


def phase_oproj(k):
    A, P, c, ps = k.A, k.P, k.c, k.ps
    lo0, hi0 = A.lo, A.hi
    oTa = A.alloc([32, TC], BF16)
    wo = Rot([A.alloc([32, 512], BF16) for _ in range(2)], "woO")
    xs_ = Rot([A.alloc([512], F32) for _ in range(3)], "xsO")
    x1o = Rot([A.alloc([512], F32) for _ in range(3)], "x1oO")
    junk = A.alloc([512], BF16)
    ssp = A.alloc([72], F32)
    tmp = A.alloc([9], F32)
    for blk in range(32):
        k.dma("sp", oTa[:, blk, :], k.dr["oT_s"][blk], (), [f"oTa{blk}"], semkey=f"oTa{blk % 4}")
    wov = k.dr["w_o"].rearrange("(ch p) d -> p ch d", p=128)
    rb = 0
    for ds in range(8):
        w, wk = wo.next()
        k.dma("pool", w, wov[:, :, ds * 512:(ds + 1) * 512], (), [wk], semkey=wk, cast=True)
        for j in range(9):
            rb = (rb + 1) % 4
            for ch in range(32):
                k.mm(ps[rb][:, :], oTa[:, ch, j * 128:(j + 1) * 128], w[:, ch, :], ch == 0, ch == 31, [wk, f"oTa{ch}"], [f"ps{rb}"])
            x, xk = xs_.next()
            k.dma("sp", x[:, :], k.dr["xo"][j][:, ds * 512:(ds + 1) * 512], (), [xk], semkey=xk)
            o, ok_ = x1o.next()
            k.tt("dve", o[:, :], ps[rb][:, :], x[:, :], ALU.add, [f"ps{rb}", xk], [ok_])
            k.act(junk[:, :], o[:, :], AF.Square, [ok_], ["junkO", f"ssp{j}_{ds}"], accum_out=ssp[:, j * 8 + ds:j * 8 + ds + 1])
            k.dma("sp", k.dr["x1_s"][j][:, ds * 512:(ds + 1) * 512], o[:, :], [ok_], ["x1_s"], semkey=ok_)
    SS = [f"ssp{j}_{ds}" for j in range(9) for ds in range(8)]
    P.op("dve", lambda h: h.tensor_reduce(out=tmp[:, :], in_=ssp[:, :].rearrange("p (j d) -> p j d", d=8), axis=AX.X, op=ALU.add), SS, ["tmpO"])
    k.rsq(c["r2c"][:, :], tmp[:, :], 1.0 / D, ["tmpO"], ["r2c"])
    P.barrier()
    A.lo, A.hi = lo0, hi0


def phase_ln2(k):
    A, P, c, ps = k.A, k.P, k.c, k.ps
    h2T = k.h2T = A.alloc_top([32, TC], BF16)
    lo0 = A.lo
    xin = Rot([A.alloc([4096], F32) for _ in range(2)], "xinL")
    xb = Rot([A.alloc([4096], BF16) for _ in range(2)], "xbL")
    rb = 0
    for j in range(9):
        x, xk = xin.next()
        k.dma("sp", x[:, :], k.dr["x1_s"][j], (), [xk], semkey=xk)
        y, yk = xb.next()
        k.act(y[:, :], x[:, :], AF.Copy, [xk, "r2c"], [yk], scale=c["r2c"][:, j:j + 1])
        for g8 in range(8):
            rb = (rb + 1) % 8
            pv = ps[rb][:, 0:256].bitcast(BF16)
            for i in range(4):
                ch = g8 * 4 + i
                k.tp(pv[:, i * 128:(i + 1) * 128], y[:, ch * 128:(ch + 1) * 128], c["identb"][:, :], [yk, "identb"], [f"ps{rb}"])
            k.tt("dve", h2T[:, g8 * 4:g8 * 4 + 4, j * 128:(j + 1) * 128], pv[:, :].rearrange("p (a t) -> p a t", t=128),
                 c["g2c"][:, g8 * 4:g8 * 4 + 4].unsqueeze(2).to_broadcast([128, 4, 128]), ALU.mult, [f"ps{rb}", "g2c"], [f"h2T{j}_{g8}"])
    P.barrier()
    A.lo = lo0


def phase_peer_q(k):
    A, P, c, ps = k.A, k.P, k.c, k.ps
    h2T = k.h2T
    lo0 = A.lo
    qpT = A.alloc([16, TC], BF16)
    skT = A.alloc([2, 8, 128], BF16)
    lo1 = A.lo
    wq = Rot([A.alloc([32, 128], BF16) for _ in range(2)], "wqP")
    for s_ in range(2):
        k.dma("pool", skT[:, s_, :, :], k.dr["skT"][s_], (), [f"skT{s_}"], semkey=f"skT{s_}", cast=True)
    bset = 0
    for blk in range(16):
        w, wk = wq.next()
        k.dma("pool", w, k.dr["wq_blk"][blk], (), [wk], semkey=wk, cast=True)
        bset ^= 1
        base = 3 * bset
        for pi, (a, b_) in enumerate(PIECES_TC):
            for ch in range(32):
                k.mm(ps[base + pi][:, 0:b_ - a], w[:, ch, :], h2T[:, ch, a:b_], ch == 0, ch == 31, [wk], [f"ps{base + pi}"])
            k.cp("act" if pi != 1 else "dve", qpT[:, blk, a:b_], ps[base + pi][:, 0:b_ - a], [f"ps{base + pi}"], [f"qpT{blk}"])
    P.barrier()
    A.lo = lo1
    s_rot = Rot([A.alloc([16, 128], F32) for _ in range(2)], "ssbP")
    m8 = A.alloc([16, 16], F32)
    tmpr = A.alloc([128], F32)
    cand = A.alloc([256], F32)
    tmpc = A.alloc([256], F32)
    negM = A.alloc([8], F32)
    Z = A.alloc([8], F32)
    lnZ = A.alloc([8], F32)
    junk = A.alloc([16], F32)
    c16a = k.c16a = A.alloc_top([9, 8, 16], F32)
    biasa = k.biasa = A.alloc_top([9, 8], F32)
    for j in range(9):
        jc = slice(j * 128, (j + 1) * 128)
        s_sb, ssk = s_rot.next()
        c16 = c16a[:, j, :, :]
        for q4 in range(4):
            for i in range(4):
                blk = q4 * 4 + i
                k.mm(ps[4 + q4][:, i * 128:(i + 1) * 128], qpT[:, blk, jc], skT[:, blk % 2, blk // 2, :], True, True, [f"qpT{blk}", f"skT{blk % 2}"], [f"ps{4 + q4}"])
            k.cp("act" if q4 % 2 == 0 else "dve", s_sb[:, q4 * 4:q4 * 4 + 4, :], ps[4 + q4][:, :].rearrange("p (a n) -> p a n", n=128), [f"ps{4 + q4}"], [f"{ssk}_{q4}"])
        SSK = [f"{ssk}_{q4}" for q4 in range(4)]
        k.dma("sp", k.dr["ss_s"][j], s_sb, SSK, [], semkey=ssk)
        for blk in range(16):
            sk = f"{ssk}_{blk // 4}"
            P.op("dve", lambda h, blk=blk, s_sb=s_sb: h.max(out=m8[:, blk, 0:8], in_=s_sb[:, blk, :]), [sk], [f"m8_{blk}"])
            P.op("dve", lambda h, blk=blk, s_sb=s_sb: h.match_replace(out=tmpr[:, :], in_to_replace=m8[:, blk, 0:8], in_values=s_sb[:, blk, :], imm_value=-1e30),
                 [sk, f"m8_{blk}"], ["tmpr"])
            P.op("dve", lambda h, blk=blk: h.max(out=m8[:, blk, 8:16], in_=tmpr[:, :]), ["tmpr"], [f"m8_{blk}"])
        for h_ in range(8):
            k.tt("dve", cand[:, :].rearrange("p (a b) -> p a b", b=16), m8[:, 2 * h_, :].unsqueeze(2).to_broadcast([128, 16, 16]),
                 m8[:, 2 * h_ + 1, :].unsqueeze(1).to_broadcast([128, 16, 16]), ALU.add, [f"m8_{2 * h_}", f"m8_{2 * h_ + 1}"], ["cand"])
            P.op("dve", lambda h, h_=h_, c16=c16: h.max(out=c16[:, h_, 0:8], in_=cand[:, :]), ["cand"], [f"c16_{h_}"])
            P.op("dve", lambda h, h_=h_, c16=c16: h.match_replace(out=tmpc[:, :], in_to_replace=c16[:, h_, 0:8], in_values=cand[:, :], imm_value=-1e30),
                 ["cand", f"c16_{h_}"], ["tmpc"])
            P.op("dve", lambda h, h_=h_, c16=c16: h.max(out=c16[:, h_, 8:16], in_=tmpc[:, :]), ["tmpc"], [f"c16_{h_}"])
        C16 = [f"c16_{h_}" for h_ in range(8)]
        k.ts("dve", negM[:, :], c16[:, :, 0], -1.0, None, ALU.mult, None, C16, ["negM"])
        for h_ in range(8):
            k.act(junk[:, :], c16[:, h_, :], AF.Exp, [f"c16_{h_}", "negM"], ["junkP", f"Z{h_}"], bias=negM[:, h_:h_ + 1], accum_out=Z[:, h_:h_ + 1])
        k.act(lnZ[:, :], Z[:, :], AF.Ln, [f"Z{h_}" for h_ in range(8)], ["lnZ"])
        k.tt("dve", biasa[:, j, :], negM[:, :], lnZ[:, :], ALU.subtract, ["negM", "lnZ"], ["biasP"])
    P.barrier()
    A.lo = lo0
    IB = 16
    NCH = k.cfg.get("nch", 128)
    s_rot = Rot([A.alloc([16, 128], F32) for _ in range(2)], "ssbG")
    sig = Rot([A.alloc([IB * 128], F32) for _ in range(3)], "sigP")
    eb_ = Rot([A.alloc([IB * 128], BF16) for _ in range(3)], "ebP")
    gh = Rot([A.alloc([IB * 128], BF16) for _ in range(4)], "ghP")
    GTst = Rot([A.alloc([16, 128], BF16) for _ in range(2)], "GTstP")
    uT = Rot([A.alloc([32, 128], BF16) for _ in range(3)], "uTU")
    gt = Rot([A.alloc([TC], BF16) for _ in range(3)], "gtU")
    gl = Rot([A.alloc([TC], F32) for _ in range(2)], "glU")
    ao = Rot([A.alloc([TC], BF16) for _ in range(3)], "aoU")
    pcnt = [0]

    postq = []
    gq = []
    GSKEW, PSKEW = 2, 2

    def u_chunk(ch_, ib):
        w, wk = uT.next()
        k.dma("sp", w, k.dr["u16_s"][ch_], (), [wk], semkey=wk)
        g, gk = gt.next()
        k.dma("sp", g[:, :], k.dr["GT_s"][ch_], [f"GTib{ib}"], [gk], semkey=gk)
        l, lk = gl.next()
        o, ok_ = ao.next()
        bks = []
        for pi, (a, b_) in enumerate(PIECES_TC):
            pcnt[0] += 1
            bk = 4 + pcnt[0] % 4
            bks.append(bk)
            for ch in range(32):
                k.mm(ps[bk][:, 0:b_ - a], w[:, ch, :], h2T[:, ch, a:b_], ch == 0, ch == 31, [wk], [f"ps{bk}"])
            if pi == 2:
                def post(bks=tuple(bks)):
                    for pi2, (a2, b2) in enumerate(PIECES_TC):
                        k.act(l[:, a2:b2], ps[bks[pi2]][:, 0:b2 - a2], AF.Gelu, [f"ps{bks[pi2]}"], [f"{lk}_{pi2}"])
                    for pi2, (a2, b2) in enumerate(PIECES_TC):
                        k.tt("dve", o[:, a2:b2], l[:, a2:b2], g[:, a2:b2], ALU.mult, [f"{lk}_{pi2}", gk], [ok_])
                    k.dma("sp", k.dr["aT_s"][ch_], o[:, :], [ok_], [], semkey=ok_)
                postq.append([2, post])
            yield

    pend = []

    def pump():
        if postq:
            postq[0][0] -= 1
            if postq[0][0] <= 0:
                postq.pop(0)[1]()
        while pend:
            try:
                next(pend[0])
                return
            except StopIteration:
                pend.pop(0)

    nib = NCH // IB
    for ib in range(nib + 1):
        for j in range(9):
            if ib >= 1:
                for cc in range((IB * j) // 9, (IB * (j + 1)) // 9):
                    pend.append(u_chunk((ib - 1) * IB + cc, ib - 1))
            if ib < nib:
                jc = slice(j * 128, (j + 1) * 128)
                s_sb, ssk = s_rot.next()
                k.dma("sp", s_sb, k.dr["ss_s"][j], (), [ssk], semkey=ssk)
                for h_ in range(8):
                    sg, sgk = sig.next()
                    k.tt("pool" if h_ != 7 else "dve", sg[:, :].rearrange("p (a b) -> p a b", b=128),
                         s_sb[:, 2 * h_, ib * IB:(ib + 1) * IB].unsqueeze(2).to_broadcast([128, IB, 128]),
                         s_sb[:, 2 * h_ + 1, :].unsqueeze(1).to_broadcast([128, IB, 128]), ALU.add, [ssk], [sgk])
                    e, ek = eb_.next()
                    k.act(e[:, :], sg[:, :], AF.Exp, [sgk], [ek], bias=biasa[:, j, h_:h_ + 1])
                    g_, gk = gh.next()
                    k.stt(g_[:, :], sg[:, :], c16a[:, j, h_, 15:16], e[:, :], ALU.is_ge, ALU.mult, [sgk, ek], [gk])

                    def gmm(g_=g_, gk=gk, h_=h_):
                        for cc in range(IB):
                            k.mm(ps[cc // 4][:, (cc % 4) * 128:(cc % 4 + 1) * 128], g_[:, cc * 128:(cc + 1) * 128], c["identb"][:, :], h_ == 0 and cc % 4 == 0, h_ == 7,
                                 [gk, "identb"], [f"ps{cc // 4}"], nocheck=True)
                    gq.append(gmm)
                    if len(gq) > GSKEW:
                        gq.pop(0)()
                    pump()
                while gq:
                    gq.pop(0)()
                st, stk = GTst.next()
                for b4 in range(IB // 4):
                    k.cp("act" if b4 % 2 == 0 else "dve", st[:, b4 * 4:b4 * 4 + 4, :], ps[b4][:, :].rearrange("p (a t) -> p a t", t=128), [f"ps{b4}"], [stk])
                k.dma("sp", k.dr["GT_s"][ib * IB:(ib + 1) * IB, :, jc].rearrange("c e t -> e c t"), st[:, :, :], [stk], [f"GTib{ib}"], semkey=stk)
            else:
                while pend:
                    pump()
    while postq:
        postq.pop(0)[1]()
    P.barrier()
    A.lo, A.hi = lo0, k.A.nbytes


def phase_peer_v(k):
    A, P, c, ps = k.A, k.P, k.c, k.ps
    lo0, hi0 = A.lo, A.hi
    NCH = k.cfg.get("nch", 128)
    NG = NCH // 8
    vc = A.alloc([NCH, 512], BF16)
    asam = A.alloc([NCH, 128], BF16)
    ag = Rot([A.alloc([8, 1024], BF16) for _ in range(2)], "agV")
    xs_ = Rot([A.alloc([512], F32) for _ in range(2)], "xsV")
    x2o = Rot([A.alloc([512], F32) for _ in range(2)], "x2oV")
    junk = A.alloc([512], BF16)
    ssp = A.alloc([72], F32)
    tmp = A.alloc([9], F32)
    vv = k.dr["peer_v"].rearrange("(c p) d -> p c d", p=128)
    for cg in range(NG):
        k.dma("sp", asam[:, cg * 8:(cg + 1) * 8, :], k.dr["aT_s"][cg * 8:(cg + 1) * 8, :, 1024:1152].rearrange("c e t -> e c t"), (), [f"asam{cg}"], semkey=f"asam{cg}")

    def evac(j, ds, bank):
        x, xk = xs_.next()
        k.dma("sp", x[:, :], k.dr["x1_s"][j][:, ds * 512:(ds + 1) * 512], (), [xk], semkey=xk)
        o, ok_ = x2o.next()
        k.tt("dve", o[:, :], ps[bank][:, :], x[:, :], ALU.add, [f"ps{bank}", xk], [ok_])
        k.act(junk[:, :], o[:, :], AF.Square, [ok_], ["junkV", f"ssv{j}_{ds}"], accum_out=ssp[:, j * 8 + ds:j * 8 + ds + 1])
        k.dma("sp", k.dr["x2_s"][j][:, ds * 512:(ds + 1) * 512], o[:, :], [ok_], [], semkey=ok_)

    for ds in range(8):
        for cg in range(NG):
            k.dma("pool", vc[:, cg * 8:(cg + 1) * 8, :], vv[:, cg * 8:(cg + 1) * 8, ds * 512:(ds + 1) * 512], (), [f"vc{cg}"], semkey=f"vc{cg}", cast=True)
            a_, ak = ag.next()
            k.dma("sp", a_, k.dr["aT_s"][cg * 8:(cg + 1) * 8, :, 0:1024].rearrange("c e t -> e c t"), (), [ak], semkey=ak)
            for j in range(8):
                for ch in range(8):
                    k.mm(ps[j][:, :], a_[:, ch, j * 128:(j + 1) * 128], vc[:, cg * 8 + ch, :], cg == 0 and ch == 0, cg == NG - 1 and ch == 7,
                         [ak, f"vc{cg}"], [f"ps{j}"])
        evac(7, ds, 7)
        for cg in range(NG):
            for ch in range(8):
                ci = cg * 8 + ch
                k.mm(ps[7][:, :], asam[:, ci, :], vc[:, ci, :], ci == 0, ci == NCH - 1, [f"asam{cg}", f"vc{cg}"], ["ps7"])
        for j in range(7):
            evac(j, ds, j)
        evac(8, ds, 7)
    SS = [f"ssv{j}_{ds}" for j in range(9) for ds in range(8)]
    P.op("dve", lambda h: h.tensor_reduce(out=tmp[:, :], in_=ssp[:, :].rearrange("p (j d) -> p j d", d=8), axis=AX.X, op=ALU.add), SS, ["tmpV"])
    k.rsq(c["r3c"][:, :], tmp[:, :], 1.0 / D, ["tmpV"], ["r3c"])
    P.barrier()
    A.lo, A.hi = lo0, hi0


def phase_final(k):
    A, P, c, ps = k.A, k.P, k.c, k.ps
    gf = A.alloc([4096], F32)
    xin = Rot([A.alloc([4096], F32) for _ in range(2)], "xinF")
    yo = Rot([A.alloc([4096], F32) for _ in range(2)], "yoF")
    k.dma("sp", gf[:, :], k.dr["gf_bc"], (), ["gf"], semkey="gf")
    for j in range(9):
        x, xk = xin.next()
        k.dma("sp", x[:, :], k.dr["x2_s"][j], (), [xk], semkey=xk)
        y, yk = yo.next()
        k.stt(y[:, :], x[:, :], c["r3c"][:, j:j + 1], gf[:, :], ALU.mult, ALU.mult, [xk, "gf", "r3c"], [yk])
        k.dma("sp", k.dr["y_own"][j], y[:, :], [yk], ["y_own"], semkey=yk)
    P.barrier()


def build(cfg):
    k = B(cfg)
    nga = cfg.get("nga", 16)
    k.din("g1c", [128, 32]); k.din("g2c", [128, 32]); k.din("gqc", [128, 8]); k.din("gkvc", [128, 4]); k.din("psc", [128, 16])
    k.din("w_in_blk", [28, 128, 32, 128]); k.din("w_pe_blk", [2, 128, 32, 64])
    k.din("xTa", [16, 128, 32, 512]); k.din("csTa", [64, 2, 8192])
    k.din("xTo", [128, 32, TE]); k.din("csTo", [64, 2, TE]); k.din("inv0", [128, 4, 144])
    k.din("spT", [2, 128, 16, 15]); k.din("pool_w_blk", [128, 64, 128]); k.din("w_uq_blk", [16, 128, 8, 256])
    k.din("w_ukv_blk", [16, 128, 4, 256]); k.din("maskT", [128, 8, 128]); k.din("cckvT", [2, 128, 4, 1024]); k.din("ckpeT", [2, 64, 1024])
    k.din("w_o", [4096, 4096]); k.din("xo", [9, 128, 4096]); k.din("wq_blk", [16, 128, 32, 128]); k.din("skT", [2, 128, 8, 128])
    k.din("uT_blk", [128, 128, 32, 128]); k.din("peer_v", [16384, 4096]); k.din("gf_bc", [128, 4096])
    k.dout("y_own", [9, 128, 4096])
    k.dscr("x1_s", [9, 128, 4096], F32); k.dscr("x2_s", [9, 128, 4096], F32)
    k.dscr("GT_s", [128, 128, TC], BF16); k.dscr("aT_s", [128, 128, TC], BF16)
    k.dscr("v16_s", [8, 16, 128, 8, 512], BF16); k.dscr("ss_s", [9, 128, 16, 128], F32); k.dscr("u16_s", [128, 128, 32, 128], BF16)
    k.din("fin_src", [1, 64]); k.dscr("fin_s", [1, 64], F32)
    k.dout("ckv_own", [9, 128, 512]); k.dout("kpe_own", [9, 128, 64]); k.dout("pool_own", [3, 15, 2048])
    k.dscr("ckvT_s", [128, 4, 8192], BF16); k.dscr("kpeT_s", [64, 8192], BF16)
    k.dscr("oT_s", [32, 128, TC], BF16); k.dscr("qT_s", [16, 128, TC], BF16); k.dscr("qpT_s", [16, 64, TC], BF16)
    phase_consts(k)
    k.P.barrier()
    if cfg.get("A", True):
        phase_A(k)
    phase_B(k)
    k.P.barrier()
    if cfg.get("upto", 99) >= 2:
        phase_attn(k)
    if cfg.get("upto", 99) >= 3:
        phase_oproj(k)
        phase_ln2(k)
    if cfg.get("upto", 99) >= 4:
        phase_peer_q(k)
    if cfg.get("upto", 99) >= 5:
        phase_peer_v(k)
        phase_final(k)
    k.P.barrier()
    fin = k.P.op("sp", lambda h: h.dma_start(out=k.dr["fin_s"][0:1, 0:64], in_=k.dr["g1c"][0:1, 0:32].bitcast(U8)[0:1, 0:64] if False else k.dr["fin_src"][0:1, 0:64]),
                 (), (), dma=True, semkey="fin")
    cnt = k.P.emit(final_wait_ops=[fin])
    k.stats = (cnt, k.P.n_sems, len(k.P.ops))
    return k


def _rope_tabs(pos):
    inv = 10000.0 ** (-2.0 * np.arange(32, dtype=np.float32) / 64).astype(np.float32)
    ang = pos.astype(np.float32)[None, :] * np.concatenate([inv, inv])[:, None]
    return np.stack([np.cos(ang), np.sin(ang)], 1).astype(np.float32)


def host_shared(inp):
    s = {}
    s["g1c"] = np.ascontiguousarray(inp["ln1_g"][0].reshape(32, 128).T)
    s["g2c"] = np.ascontiguousarray(inp["ln2_g"][0].reshape(32, 128).T)
    s["gqc"] = np.ascontiguousarray(inp["q_norm_g"][0].reshape(8, 128).T)
    s["gkvc"] = np.ascontiguousarray(inp["kv_norm_g"][0].reshape(4, 128).T)
    s["psc"] = np.ascontiguousarray(inp["pool_scale"][0].reshape(16, 128).T)
    W = inp["w_in"][0]
    sel = np.r_[0:1536, 1600:3648]
    s["w_in_blk"] = np.ascontiguousarray(W[:, sel].reshape(32, 128, 28, 128).transpose(2, 1, 0, 3))
    pe = W[:, 1536:1600]
    pr = np.concatenate([pe[:, 32:64], pe[:, 0:32]], 1)
    s["w_pe_blk"] = np.ascontiguousarray(np.stack([pe, pr]).reshape(2, 32, 128, 64).transpose(0, 2, 1, 3))
    xp = inp["x_prompt"][0]
    s["xTa"] = np.ascontiguousarray(xp.reshape(16, 512, 32, 128).transpose(0, 3, 2, 1))
    s["csTa"] = _rope_tabs(np.arange(8192))
    s["pool_w_blk"] = np.ascontiguousarray(inp["pool_w"][0].reshape(4, 4, 128, 4, 128).transpose(2, 0, 3, 1, 4).reshape(128, 64, 128))
    Wq = inp["w_uq"][0]
    qb = np.concatenate([Wq[:, :, 0:128], Wq[:, :, 128:192], Wq[:, :, 160:192], Wq[:, :, 128:160]], 2)
    s["w_uq_blk"] = np.ascontiguousarray(qb.reshape(8, 128, 16, 256).transpose(2, 1, 0, 3))
    s["fin_src"] = np.zeros((1, 64), np.float32)
    s["w_o"] = inp["w_o"][0]
    s["wq_blk"] = np.ascontiguousarray(inp["peer_wq"][0].reshape(32, 128, 16, 128).transpose(2, 1, 0, 3))
    s["skT"] = np.ascontiguousarray(np.stack([inp["peer_sk1"][0].transpose(2, 0, 1), inp["peer_sk2"][0].transpose(2, 0, 1)]))
    s["uT_blk"] = np.ascontiguousarray(inp["peer_u"][0].reshape(128, 128, 32, 128).transpose(0, 3, 2, 1))
    s["peer_v"] = inp["peer_v"][0]
    s["gf_bc"] = np.ascontiguousarray(np.broadcast_to(inp["final_g"][None, :], (128, 4096)))
    s["w_ukv_blk"] = np.ascontiguousarray(inp["w_ukv"][0].transpose(1, 0, 2).reshape(16, 4, 128, 256).transpose(0, 2, 1, 3))
    return s


def host_core(c, inp):
    m = {}
    xp = inp["x_prompt"][0]
    xs = inp["x_sample"]
    cols, pos = [], []
    for j in range(8):
        s0 = 128 * (8 * j + c)
        if s0 >= 16:
            cols.append(xp[s0 - 16:s0])
        else:
            cols.append(np.zeros((16, D), np.float32))
        cols.append(xp[s0:s0 + 128])
        pos += list(range(s0 - 16, s0 + 128))
    cols += [xs[2 * c], xs[2 * c + 1]]
    pos += list(range(1024, 1088)) * 2
    xext = np.concatenate(cols, 0)
    m["xTo"] = np.ascontiguousarray(xext.T.reshape(32, 128, TE).transpose(1, 0, 2))
    m["xo"] = np.ascontiguousarray(np.stack([xp[128 * (8 * j + c):128 * (8 * j + c) + 128] for j in range(8)] + [np.concatenate([xs[2 * c], xs[2 * c + 1]], 0)]))
    m["csTo"] = _rope_tabs(np.maximum(np.array(pos), 0))
    inv0 = np.zeros((4, 144), np.float32)
    for g in range(4):
        w = 2 << g
        p0 = 128 * c - 16 + np.arange(144)
        inv0[g] = 1.0 / np.minimum(np.maximum(p0, 0) + 1, w)
    m["inv0"] = np.ascontiguousarray(np.broadcast_to(inv0[None], (128, 4, 144)))
    mk = np.zeros((128, 8, 128), np.float32)
    for kb in range(8):
        if kb < c:
            mk[:, kb, :] = 1.0
        elif kb == c:
            ss = np.arange(128)[:, None] // 64
            tt = np.arange(128)[None, :] // 64
            mk[:, kb, :] = (tt >= ss).astype(np.float32)
    m["maskT"] = mk
    m["cckvT"] = np.ascontiguousarray(np.stack([inp["cache_ckv"][0, 2 * c + b].T.reshape(4, 128, 1024).transpose(1, 0, 2) for b in range(2)]))
    m["ckpeT"] = np.ascontiguousarray(np.stack([inp["cache_kpe"][0, 2 * c + b].T for b in range(2)]))
    sp = inp["state_pool"][0]
    m["spT"] = np.ascontiguousarray(np.stack([sp[2 * c + b].T.reshape(16, 128, 15).transpose(1, 0, 2) for b in range(2)]))
    return m


def kernel(**inputs):
    inp = {k_: np.asarray(v) for k_, v in inputs.items()}
    k = build({})
    sh = host_shared(inp)
    maps = []
    for c in range(NCORES):
        m = dict(sh)
        m.update(host_core(c, inp))
        maps.append(m)
    res = run_bass_kernel_spmd(k.nc, maps, core_ids=list(range(NCORES)))
    R = res.results
    y_p = np.zeros((1, 8192, D), np.float32)
    y_s = np.zeros((16, 64, D), np.float32)
    ckv_p = np.zeros((1, 1, 8192, 512), np.float32)
    kpe_p = np.zeros((1, 1, 8192, 64), np.float32)
    ckv_s = np.zeros((1, 16, 64, 512), np.float32)
    kpe_s = np.zeros((1, 16, 64, 64), np.float32)
    pool_s = np.zeros((1, 16, 15, 2048), np.float32)
    for c in range(NCORES):
        y = np.asarray(R[c]["y_own"])
        ck = np.asarray(R[c]["ckv_own"])
        kp = np.asarray(R[c]["kpe_own"])
        po = np.asarray(R[c]["pool_own"])
        for j in range(8):
            s0 = 128 * (8 * j + c)
            y_p[0, s0:s0 + 128] = y[j]
            ckv_p[0, 0, s0:s0 + 128] = ck[j]
            kpe_p[0, 0, s0:s0 + 128] = kp[j]
        for b in range(2):
            y_s[2 * c + b] = y[8, 64 * b:64 * b + 64]
            ckv_s[0, 2 * c + b] = ck[8, 64 * b:64 * b + 64]
            kpe_s[0, 2 * c + b] = kp[8, 64 * b:64 * b + 64]
            pool_s[0, 2 * c + b] = po[1 + b]
    pool_p = np.asarray(R[7]["pool_own"])[0][None, None].astype(np.float32)
    return (y_p, y_s, ckv_p, kpe_p, pool_p, ckv_s, kpe_s, pool_s)
```

```python
import numpy as np
import concourse.bass as bass
import concourse.mybir as mybir
from concourse.bass_utils import run_bass_kernel_spmd

F32 = mybir.dt.float32
BF16 = mybir.dt.bfloat16
U8 = mybir.dt.uint8
AF = mybir.ActivationFunctionType
ALU = mybir.AluOpType
AX = mybir.AxisListType

NCORES = 8
D = 4096
EPS = 1e-6
SCALE = 192 ** -0.5
TE = 1280
TC = 1152
TU = 1312
PIECES_TE = ((0, 512), (512, 1024), (1024, 1280))
PIECES_TC = ((0, 512), (512, 1024), (1024, 1152))
ENGS = ("pe", "dve", "act", "pool", "sp")


class Prog:
    def __init__(self, nc, same_engine_sync=True):
        self.nc = nc
        self.ops = []
        self.last_w = {}
        self.readers = {}
        self.same_engine_sync = same_engine_sync
        self.last_eng = {}
        self.dma_since = {}
        self.pending = {}
        self.slots = {}

    def op(self, eng, fn, reads=(), writes=(), dma=False, semkey=None):
        i = len(self.ops)
        deps = set()
        for r in reads:
            if r in self.last_w:
                deps.add(self.last_w[r])
        for w in writes:
            if w in self.last_w:
                deps.add(self.last_w[w])
            for rd in self.readers.get(w, ()):
                deps.add(rd)
        if eng in self.pending:
            deps |= self.pending.pop(eng)
        deps.discard(i)
        for r in reads:
            lst = self.readers.setdefault(r, [])
            if not dma and lst and (not self.ops[lst[-1]]["dma"]) and self.ops[lst[-1]]["eng"] == eng:
                lst[-1] = i
            else:
                lst.append(i)
        for w in writes:
            self.last_w[w] = i
            self.readers[w] = []
        if dma:
            if semkey is None:
                semkey = ("dma",) + tuple(writes) + tuple(reads)
            semkey = self.slots.setdefault(semkey, len(self.slots))
            self.dma_since[semkey] = i
        else:
            self.last_eng[eng] = i
        self.ops.append(dict(eng=eng, fn=fn, dma=dma, semkey=semkey, deps=deps))
        return i

    def barrier(self):
        deps = set(self.last_eng.values()) | set(self.dma_since.values())
        for e in ENGS:
            self.pending[e] = set(deps) | self.pending.get(e, set())
        self.dma_since = {}
        self.last_w = {}
        self.readers = {}
        self.slots = {}

    def emit(self, final_wait_ops=()):
        nc = self.nc
        ops = self.ops

        def skip(p, o):
            if p["dma"] or o["dma"]:
                return False
            if p["eng"] != o["eng"]:
                return False
            return p["eng"] == "pe" or not self.same_engine_sync

        needed = set()
        for i, o in enumerate(ops):
            for d in o["deps"]:
                if not skip(ops[d], o):
                    needed.add(d)
        for d in final_wait_ops:
            needed.add(d)
        eng_cnt = {e: 0 for e in ENGS}
        dma_cnt = {}
        sig = {}
        dma_keys = []
        for i, o in enumerate(ops):
            if o["dma"]:
                k = o["semkey"]
                if k not in dma_cnt:
                    dma_cnt[k] = 0
                    dma_keys.append(k)
                dma_cnt[k] += 16
                sig[i] = (("dma", k), dma_cnt[k])
            elif i in needed:
                eng_cnt[o["eng"]] += 1
                sig[i] = (("eng", o["eng"]), eng_cnt[o["eng"]])
        sems = {}
        for e in ENGS:
            sems[("eng", e)] = nc.alloc_semaphore(name=f"s_{e}")
        for n, k in enumerate(dma_keys):
            sems[("dma", k)] = nc.alloc_semaphore(name=f"d_{n}")
        self.n_sems = len(sems)
        per_eng = {e: [] for e in ENGS}
        for i, o in enumerate(ops):
            per_eng[o["eng"]].append(i)
        waited = {e: {} for e in ENGS}

        def run(e, h):
            wd = waited[e]
            for i in per_eng[e]:
                o = ops[i]
                req = {}
                for d in o["deps"]:
                    if d not in sig or skip(ops[d], o):
                        continue
                    sk, v = sig[d]
                    if v > req.get(sk, 0):
                        req[sk] = v
                for sk, v in req.items():
                    if wd.get(sk, 0) >= v:
                        continue
                    h.wait_ge(sems[sk], v)
                    wd[sk] = v
                ins = o["fn"](h)
                if i in sig:
                    sk, v = sig[i]
                    ins.then_inc(sems[sk], 16 if o["dma"] else 1)
            if e == "sp":
                for d in final_wait_ops:
                    sk, v = sig[d]
                    if wd.get(sk, 0) < v:
                        h.wait_ge(sems[sk], v)
                        wd[sk] = v

        with nc.Block() as block:
            @block.tensor
            def _(h):
                run("pe", h)

            @block.vector
            def _(h):
                run("dve", h)

            @block.scalar
            def _(h):
                run("act", h)

            @block.gpsimd
            def _(h):
                run("pool", h)

            @block.sync
            def _(h):
                run("sp", h)
        return eng_cnt


ISZ = {F32: 4, BF16: 2, U8: 1}


class Arena:
    def __init__(self, nc, nbytes):
        self.t = nc.alloc_sbuf_tensor("arena", [128, nbytes], U8)
        self.nbytes = nbytes
        self.lo = 0
        self.hi = nbytes

    def _view(self, off, shape, dtype):
        n = int(np.prod(shape))
        ap = self.t[:, off:off + n * ISZ[dtype]].bitcast(dtype)
        if len(shape) == 1:
            return ap
        names = " ".join(f"d{i}" for i in range(len(shape)))
        kw = {f"d{i}": int(s) for i, s in enumerate(shape)}
        return ap.rearrange(f"p ({names}) -> p {names}", **kw)

    def alloc(self, shape, dtype):
        off = (self.lo + 63) // 64 * 64
        n = int(np.prod(shape)) * ISZ[dtype]
        assert off + n <= self.hi, f"arena overflow: need {off + n} have {self.hi}"
        self.lo = off + n
        return self._view(off, shape, dtype)

    def alloc_top(self, shape, dtype):
        n = int(np.prod(shape)) * ISZ[dtype]
        off = (self.hi - n) // 64 * 64
        assert off >= self.lo, "arena overflow (top)"
        self.hi = off
        return self._view(off, shape, dtype)


class Rot:
    def __init__(self, aps, name):
        self.aps = aps
        self.name = name
        self.i = -1

    def next(self):
        self.i += 1
        k = self.i % len(self.aps)
        return self.aps[k], f"{self.name}{k}"


class B:
    def __init__(self, cfg):
        self.cfg = cfg
        nc = self.nc = bass.Bass("TRN2", target_bir_lowering=False)
        self.P = Prog(nc)
        self.dr = {}
        self.outs = []
        self.A = Arena(nc, 207 * 1024)
        self.ps = [nc.alloc_psum_tensor(f"psb{i}", [128, 512], F32) for i in range(8)]
        self.psrot = 0

    def din(self, name, shape, dt=F32):
        self.dr[name] = self.nc.dram_tensor(name, list(shape), dt, kind="ExternalInput").ap()
        return self.dr[name]

    def dout(self, name, shape, dt=F32):
        self.dr[name] = self.nc.dram_tensor(name, list(shape), dt, kind="ExternalOutput").ap()
        return self.dr[name]

    def dscr(self, name, shape, dt):
        kind = "ExternalOutput" if name in self.cfg.get("debug", ()) else "Internal"
        self.dr[name] = self.nc.dram_tensor(name, list(shape), dt, kind=kind).ap()
        return self.dr[name]

    def mm(self, out, lhsT, rhs, start, stop, r, w, nocheck=False):
        if nocheck:
            return self.P.op("pe", lambda h: h.matmul(out, lhsT=lhsT, rhs=rhs, start=start, stop=stop, skip_group_check=True), r, w)
        return self.P.op("pe", lambda h: h.matmul(out, lhsT=lhsT, rhs=rhs, start=start, stop=stop), r, w)

    def tp(self, out, in_, ident, r, w):
        return self.P.op("pe", lambda h: h.transpose(out=out, in_=in_, identity=ident), r, w)

    def act(self, out, in_, func, r, w, **kw):
        return self.P.op("act", lambda h: h.activation(out=out, in_=in_, func=func, **kw), r, w)

    def tt(self, eng, out, in0, in1, op, r, w):
        return self.P.op(eng, lambda h: h.tensor_tensor(out=out, in0=in0, in1=in1, op=op), r, w)

    def ts(self, eng, out, in0, s1, s2, op0, op1, r, w):
        if op1 is None:
            return self.P.op(eng, lambda h: h.tensor_scalar(out=out, in0=in0, scalar1=s1, scalar2=None, op0=op0), r, w)
        return self.P.op(eng, lambda h: h.tensor_scalar(out=out, in0=in0, scalar1=s1, scalar2=s2, op0=op0, op1=op1), r, w)

    def stt(self, out, in0, scalar, in1, op0, op1, r, w):
        return self.P.op("dve", lambda h: h.scalar_tensor_tensor(out=out, in0=in0, scalar=scalar, in1=in1, op0=op0, op1=op1), r, w)

    def cp(self, eng, out, in_, r, w):
        if eng == "act":
            return self.P.op("act", lambda h: h.copy(out=out, in_=in_), r, w)
        return self.P.op(eng, lambda h: h.tensor_copy(out=out, in_=in_), r, w)

    def dma(self, q, out, in_, r, w, semkey, cast=False):
        r = [x for x in r if x not in self.dr]
        w = [x for x in w if x not in self.dr]
        if cast:
            return self.P.op("pool", lambda h: h.dma_start(out=out, in_=in_, max_dma_last_dim=4096), r, w, dma=True, semkey=semkey)
        return self.P.op(q, lambda h: h.dma_start(out=out, in_=in_), r, w, dma=True, semkey=semkey)

    def rsq(self, out, in_, inv_n, r, w):
        self.act(out, in_, AF.Sqrt, r + ["epsc"], w, scale=inv_n, bias=self.c["epsc"][0:out.shape[0], 0:1])
        self.P.op("dve", lambda h: h.reciprocal(out=out, in_=out), w, w)

    def bank(self):
        self.psrot = (self.psrot + 1) % 8
        return self.psrot


def ext_cols(j):
    if j < 8:
        return 144 * j + 16, 144 * j + 144
    return (1152, 1280)


def phase_consts(k):
    A, P = k.A, k.P
    c = k.c = {}
    c["identf"] = A.alloc([128], F32)
    c["identb"] = A.alloc([128], BF16)
    c["onesb"] = A.alloc([128], BF16)
    c["g1c"] = A.alloc([32], F32)
    c["g2c"] = A.alloc([32], F32)
    c["gqc"] = A.alloc([8], F32)
    c["gkvc"] = A.alloc([4], F32)
    c["psc"] = A.alloc([16], F32)
    c["ckvTs"] = A.alloc([4, 128], BF16)
    c["kpeTs"] = A.alloc([128], BF16)
    c["r2c"] = A.alloc([9], F32)
    c["r3c"] = A.alloc([9], F32)
    c["epsc"] = A.alloc([1], F32)
    P.op("pool", lambda h: h.memset(c["epsc"][:, :], EPS), (), ["epsc"])
    idf = c["identf"]
    P.op("pool", lambda h: h.memset(idf[:, :], 1.0), (), ["identf"])
    P.op("pool", lambda h: h.affine_select(out=idf[:, :], in_=idf[:, :], pattern=[[-1, 128]], compare_op=ALU.is_equal,
                                           fill=0.0, base=0, channel_multiplier=1), ["identf"], ["identf"])
    k.cp("dve", c["identb"][:, :], idf[:, :], ["identf"], ["identb"])
    P.op("pool", lambda h: h.memset(c["onesb"][:, :], 1.0), (), ["onesb"])
    for nm in ("g1c", "g2c", "gqc", "gkvc", "psc"):
        k.dma("sp", c[nm][:, :], k.dr[nm][:, :], (), [nm], semkey="c_" + nm)


def load_w_block(k, dst, src, M, nch, gcol, stage, skey, r, w):
    k.dma("sp", stage[:, 0:nch, 0:M], src, (), [skey], semkey=skey)
    k.tt("dve", dst, stage[:, 0:nch, 0:M], gcol.unsqueeze(2).to_broadcast([128, nch, M]), ALU.mult, [skey] + r, w)


def phase_A(k):
    A, P, c, ps = k.A, k.P, k.c, k.ps
    lo0, hi0 = A.lo, A.hi
    ng = k.cfg.get("nga", 16)
    wkv = A.alloc([32, 640], BF16)
    lo_w = (A.lo + 63) // 64 * 64
    wst = A.alloc([32, 128], F32)
    A.lo = lo_w
    sqall = A.alloc([32, 512], BF16)
    xa = Rot([A.alloc([32, 512], BF16) for _ in range(2)], "xa")
    r1 = A.alloc([512], F32)
    zk = A.alloc([4, 512], F32)
    sq2 = Rot([A.alloc([512], BF16) for _ in range(2)], "sq2a")
    rkv = A.alloc([512], F32)
    co = Rot([A.alloc([4, 512], BF16) for _ in range(2)], "coa")
    cs = Rot([A.alloc([2, 512], F32) for _ in range(2)], "csa")
    t1 = A.alloc([512], F32)
    t2 = A.alloc([512], F32)
    ko = Rot([A.alloc([512], BF16) for _ in range(2)], "koa")
    wblk, wpe = k.dr["w_in_blk"], k.dr["w_pe_blk"]
    for i in range(4):
        load_w_block(k, wkv[:, :, i * 128:(i + 1) * 128], wblk[8 + i], 128, 32, c["g1c"], wst, "wst", ["g1c"], [f"wkv{i}"])
    for i in range(2):
        load_w_block(k, wkv[:, :, 512 + 64 * i:576 + 64 * i], wpe[i], 64, 32, c["g1c"], wst, "wst", ["g1c"], [f"wkv{4 + i}"])
    P.op("act", lambda h: h.mul(out=wkv[:, :, 576:608], in_=wkv[:, :, 576:608], mul=-1.0), ["wkv5"], ["wkv5"])
    WK = [f"wkv{i}" for i in range(6)]
    zr = Rot([A.alloc([6, 512], F32) for _ in range(2)], "zra")
    xs, xks = [], []

    def load(g):
        x, xk = xa.next()
        k.dma("pool", x, k.dr["xTa"][g], (), [xk], semkey=xk, cast=True)
        xs.append(x)
        xks.append(xk)

    load(0)
    for g in range(ng):
        x, xk = xs[g], xks[g]
        if g + 1 < ng:
            load(g + 1)
        for q4 in range(4):
            k.act(sqall[:, q4 * 8:(q4 + 1) * 8, :], x[:, q4 * 8:(q4 + 1) * 8, :], AF.Square, [xk], [f"sqall{q4}"] + (["wst"] if g == 0 else []))
        z, zk_ = zr.next()
        for eb in range(4):
            for ch in range(32):
                k.mm(ps[1 + eb][:, :], wkv[:, ch, eb * 128:(eb + 1) * 128], x[:, ch, :], ch == 0, ch == 31, [xk] + WK, [f"psA{1 + eb}"])
            k.cp("act" if eb % 2 == 0 else "dve", z[:, eb, :], ps[1 + eb][:, :], [f"psA{1 + eb}"], [f"{zk_}_{eb}"])
        for i in range(2):
            for ch in range(32):
                k.mm(ps[5 + i][0:64, :], wkv[:, ch, 512 + 64 * i:576 + 64 * i], x[:, ch, :], ch == 0, ch == 31, [xk] + WK, [f"psA{5 + i}"])
            k.cp("dve" if i == 0 else "act", z[0:64, 4 + i, :], ps[5 + i][0:64, :], [f"psA{5 + i}"], [f"{zk_}_{4 + i}"])
        for ch in range(32):
            k.mm(ps[0][:, :], c["onesb"][:, :], sqall[:, ch, :], ch == 0, ch == 31, [f"sqall{ch // 8}", "onesb"], ["psA0"])
        k.rsq(r1[:, :], ps[0][:, :], 1.0 / D, ["psA0"], ["r1a"])
        for eb in range(4):
            k.tt("dve" if eb % 2 == 0 else "pool", zk[:, eb, :], z[:, eb, :], r1[:, :], ALU.mult, [f"{zk_}_{eb}", "r1a"], [f"zk{eb}"])
            s, sk = sq2.next()
            k.act(s[:, :], zk[:, eb, :], AF.Square, [f"zk{eb}"], [sk])
            k.mm(ps[7][:, :], c["onesb"][:, :], s[:, :], eb == 0, eb == 3, [sk, "onesb"], ["psA7"])
        k.rsq(rkv[:, :], ps[7][:, :], 1.0 / 512, ["psA7"], ["rkva"])
        o, ok_ = co.next()
        for eb in range(4):
            k.stt(o[:, eb, :], zk[:, eb, :], c["gkvc"][:, eb:eb + 1], rkv[:, :], ALU.mult, ALU.mult,
                  [f"zk{eb}", "rkva", "gkvc"], [ok_])
        k.dma("sp", k.dr["ckvT_s"][:, :, g * 512:(g + 1) * 512], o, [ok_], [], semkey=ok_)
        t, tk = cs.next()
        k.dma("sp", t[0:64, :, :], k.dr["csTa"][:, :, g * 512:(g + 1) * 512], (), [tk], semkey=tk)
        k.tt("dve", t1[0:64, :], z[0:64, 4, :], t[0:64, 0, :], ALU.mult, [f"{zk_}_4", tk], ["t1a"])
        k.tt("pool", t2[0:64, :], z[0:64, 5, :], t[0:64, 1, :], ALU.mult, [f"{zk_}_5", tk], ["t2a"])
        k.tt("pool", t1[0:64, :], t1[0:64, :], t2[0:64, :], ALU.add, ["t1a", "t2a"], ["t1a"])
        o2, ok2 = ko.next()
        k.tt("dve", o2[0:64, :], t1[0:64, :], r1[0:64, :], ALU.mult, ["t1a", "r1a"], [ok2])
        k.dma("sp", k.dr["kpeT_s"][:, g * 512:(g + 1) * 512], o2[0:64, :], [ok2], [], semkey=ok2)
    P.barrier()
    A.lo, A.hi = lo0, hi0


def phase_B(k):
    A, P, c, ps = k.A, k.P, k.c, k.ps
    lo0, hi0 = A.lo, A.hi
    wblk, wpe = k.dr["w_in_blk"], k.dr["w_pe_blk"]
    ones = c["onesb"]
    xT = A.alloc([32, TE], BF16)
    r1 = A.alloc([TE], F32)
    wst = A.alloc([32, 128], F32)
    wb = Rot([A.alloc([32, 128], BF16) for _ in range(2)], "wbB")
    sq = Rot([A.alloc([512], BF16) for _ in range(3)], "sqB")
    nqT = A.alloc_top([8, TE], BF16)
    lo1 = A.lo
    for q4 in range(4):
        k.dma("pool", xT[:, q4 * 8:(q4 + 1) * 8, :], k.dr["xTo"][:, q4 * 8:(q4 + 1) * 8, :], (), [f"xT{q4}"], semkey=f"xT{q4}", cast=True)
    cnt = [0]

    def ssq_bc(srcs, n_src, out, inv_n, okey):
        for pi, (a, b_) in enumerate(PIECES_TE):
            cnt[0] += 1
            bk = 6 + (cnt[0] % 2)
            for i in range(n_src):
                ap, keys = srcs(i, a, b_)
                s, sk = sq.next()
                k.act(s[:, 0:b_ - a], ap, AF.Square, keys, [sk])
                k.mm(ps[bk][:, 0:b_ - a], ones[:, :], s[:, 0:b_ - a], i == 0, i == n_src - 1, [sk, "onesb"], [f"ps{bk}"])
            k.rsq(out[:, a:b_], ps[bk][:, 0:b_ - a], inv_n, [f"ps{bk}"], [f"{okey}{pi}"])

    ssq_bc(lambda ch, a, b_: (xT[:, ch, a:b_], [f"xT{ch // 8}"]), 32, r1, 1.0 / D, "r1_")
    R1 = ["r1_0", "r1_1", "r1_2"]
    bset = [0]

    ucast = [0]

    def zblock(src, M, post, neg=None):
        w, wk = wb.next()
        load_w_block(k, w[:, :, 0:M], src, M, 32, c["g1c"], wst, "wstB", ["g1c"], [wk])
        if neg is not None:
            P.op("act", lambda h: h.mul(out=w[:, :, neg[0]:neg[1]], in_=w[:, :, neg[0]:neg[1]], mul=-1.0), [wk], [wk])
        bset[0] ^= 1
        base = 3 * bset[0]
        for pi, (a, b_) in enumerate(PIECES_TE):
            pk = f"ps{base + pi}"
            for ch in range(32):
                k.mm(ps[base + pi][0:M, 0:b_ - a], w[:, ch, 0:M], xT[:, ch, a:b_], ch == 0, ch == 31, [wk, f"xT{ch // 8}"], [pk])
            post(pi, a, b_, ps[base + pi][0:M, 0:b_ - a], pk)

    if k.cfg.get("B1", True):
        for eb in range(8):
            def post(pi, a, b_, p, pk, eb=eb):
                k.tt("dve", nqT[:, eb, a:b_], p, r1[:, a:b_], ALU.mult, [pk, f"r1_{pi}"], [f"nq{eb}_{pi}"])
            zblock(wblk[eb], 128, post)
        rq = A.alloc([TE], F32)
        ssq_bc(lambda eb, a, b_: (nqT[:, eb, a:b_], [f"nq{eb}_{PIECES_TE.index((a, b_))}"]), 8, rq, 1.0 / 1024, "rq_")
        for eb in range(8):
            for pi, (a, b_) in enumerate(PIECES_TE):
                k.tt("dve", nqT[:, eb, a:b_], nqT[:, eb, a:b_], rq[:, a:b_], ALU.mult, [f"nq{eb}_{pi}", f"rq_{pi}"], [f"nq{eb}_{pi}"])
    A.lo = lo1
    zk = A.alloc([4, TE], F32)
    rkv = A.alloc([TE], F32)
    pe = A.alloc([2, TE], F32)
    cs = A.alloc([2, TE], F32)
    ost = Rot([A.alloc([576], F32) for _ in range(2)], "ostB")
    for eb in range(4):
        def post(pi, a, b_, p, pk, eb=eb):
            k.tt("dve", zk[:, eb, a:b_], p, r1[:, a:b_], ALU.mult, [pk, f"r1_{pi}"], [f"zkB{eb}_{pi}"])
        zblock(wblk[8 + eb], 128, post)
    ssq_bc(lambda eb, a, b_: (zk[:, eb, a:b_], [f"zkB{eb}_{PIECES_TE.index((a, b_))}"]), 4, rkv, 1.0 / 512, "rkvB")
    for eb in range(4):
        for pi, (a, b_) in enumerate(PIECES_TE):
            k.stt(zk[:, eb, a:b_], zk[:, eb, a:b_], c["gkvc"][:, eb:eb + 1], rkv[:, a:b_], ALU.mult, ALU.mult,
                  [f"zkB{eb}_{pi}", f"rkvB{pi}", "gkvc"], [f"zkB{eb}_{pi}"])
        k.cp("pool", c["ckvTs"][:, eb, :], zk[:, eb, 1152:1280], [f"zkB{eb}_2"], [f"ckvTs{eb}"])
    k.dma("sp", cs[0:64, :, :], k.dr["csTo"][:, :, :], (), ["csB"], semkey="csB")
    for i in range(2):
        def post(pi, a, b_, p, pk, i=i):
            k.tt("dve", pe[0:64, i, a:b_], p, cs[0:64, i, a:b_], ALU.mult, [pk, "csB"], [f"peB{i}_{pi}"])
        zblock(wpe[i], 64, post, neg=(0, 32) if i == 1 else None)
    for pi, (a, b_) in enumerate(PIECES_TE):
        k.tt("pool", pe[0:64, 0, a:b_], pe[0:64, 0, a:b_], pe[0:64, 1, a:b_], ALU.add, [f"peB0_{pi}", f"peB1_{pi}"], [f"peB0_{pi}"])
        k.tt("dve", pe[0:64, 0, a:b_], pe[0:64, 0, a:b_], r1[0:64, a:b_], ALU.mult, [f"peB0_{pi}", f"r1_{pi}"], [f"peB0_{pi}"])
    k.cp("pool", c["kpeTs"][0:64, :], pe[0:64, 0, 1152:1280], ["peB0_2"], ["kpeTs"])
    ZK = [f"zkB{eb}_{pi}" for eb in range(4) for pi in range(3)]
    PE0 = [f"peB0_{pi}" for pi in range(3)]
    for j in range(9):
        a, b_ = ext_cols(j)
        o, ok_ = ost.next()
        bk = 6 + (j % 2)
        for eb in range(4):
            k.tp(ps[bk][:, eb * 128:(eb + 1) * 128], zk[:, eb, a:b_], c["identf"][:, :], ZK + ["identf"], [f"ps{bk}"])
        k.cp("act", o[:, 0:512], ps[bk][:, :], [f"ps{bk}"], [ok_ + "a"])
        k.dma("sp", k.dr["ckv_own"][j], o[:, 0:512], [ok_ + "a"], ["ckv_own"], semkey=ok_ + "a")
        bk2 = 6 + ((j + 1) % 2)
        k.tp(ps[bk2][:, 0:64], pe[0:64, 0, a:b_], c["identf"][0:64, 0:64], PE0 + ["identf"], [f"ps{bk2}"])
        k.cp("dve", o[:, 512:576], ps[bk2][:, 0:64], [f"ps{bk2}"], [ok_ + "b"])
        k.dma("sp", k.dr["kpe_own"][j], o[:, 512:576], [ok_ + "b"], ["kpe_own"], semkey=ok_ + "b")
    A.lo = lo1
    if k.cfg.get("B3", True):
        dT = A.alloc([16, TC], BF16)
        lo_d = A.lo
        u = Rot([A.alloc([TU], F32) for _ in range(2)], "uB")
        sb = [A.alloc([TU], F32) for _ in range(2)]
        inv0 = A.alloc([4, 144], F32)
        pst = Rot([A.alloc([3, 128], F32) for _ in range(2)], "pstB")
        tmp0 = A.alloc([128], F32)
        k.dma("sp", inv0, k.dr["inv0"], (), ["inv0"], semkey="inv0")
        for blk in range(16):
            g = blk // 4
            w_ = 2 << g
            uu, uk = u.next()

            def post(pi, a, b_, p, pk, uu=uu, uk=uk):
                if pi < 2:
                    k.tt("dve", uu[:, a:b_], p, r1[:, a:b_], ALU.mult, [pk, f"r1_{pi}"], [f"{uk}_{pi}"])
                else:
                    k.tt("dve", uu[:, 1024:1152], p[:, 0:128], r1[:, 1024:1152], ALU.mult, [pk, "r1_2"], [f"{uk}_2"])
                    k.tt("dve", uu[:, 1168:1232], p[:, 128:192], r1[:, 1152:1216], ALU.mult, [pk, "r1_2"], [f"{uk}_3"])
                    k.tt("dve", uu[:, 1248:1312], p[:, 192:256], r1[:, 1216:1280], ALU.mult, [pk, "r1_2"], [f"{uk}_4"])
            zblock(wblk[12 + blk], 128, post)
            k.dma("sp", uu[:, 1153:1168], k.dr["spT"][0][:, blk, :], (), [f"{uk}_5"], semkey=f"{uk}_5")
            k.dma("sp", uu[:, 1233:1248], k.dr["spT"][1][:, blk, :], (), [f"{uk}_6"], semkey=f"{uk}_6")
            UK = [f"{uk}_{i}" for i in range(7)]
            cur, ck = uu, UK
            for st in range(g + 1):
                sh = 1 << st
                nxt = sb[st % 2]
                k.tt("pool" if st % 2 == 0 else "dve", nxt[:, sh:TU], cur[:, sh:TU], cur[:, 0:TU - sh], ALU.add, ck, [f"sB{st % 2}"])
                cur, ck = nxt, [f"sB{st % 2}"]
            S = cur
            k.stt(dT[:, blk, 0:1024].rearrange("p (j t) -> p j t", t=128), S[:, 0:1152].rearrange("p (j t) -> p j t", t=144)[:, :, 16:144],
                  1.0 / w_, uu[:, 0:1152].rearrange("p (j t) -> p j t", t=144)[:, :, 16:144], ALU.mult, ALU.subtract, ck + UK, [f"dT{blk}"])
            k.stt(dT[:, blk, 1024:1088], S[:, 1168:1232], 1.0 / w_, uu[:, 1168:1232], ALU.mult, ALU.subtract, ck + UK, [f"dT{blk}"])
            k.stt(dT[:, blk, 1088:1152], S[:, 1248:1312], 1.0 / w_, uu[:, 1248:1312], ALU.mult, ALU.subtract, ck + UK, [f"dT{blk}"])
            k.tt("dve", tmp0[:, :], S[:, 16:144], inv0[:, g, 16:144], ALU.mult, ck + ["inv0"], ["tmp0"])
            k.tt("dve", dT[:, blk, 0:128], tmp0[:, :], uu[:, 16:144], ALU.subtract, ["tmp0"] + UK, [f"dT{blk}"])
            o, ok_ = pst.next()
            for i, c0 in enumerate((1137, 1217, 1297)):
                k.tp(ps[7][0:15, i * 128:(i + 1) * 128], uu[:, c0:c0 + 15], c["identf"][:, :], UK + ["identf"], ["ps7"])
            k.cp("act", o[0:15, :, :], ps[7][0:15, 0:384].rearrange("p (a b) -> p a b", b=128), ["ps7"], [ok_])
            k.dma("sp", k.dr["pool_own"][:, :, blk * 128:(blk + 1) * 128].rearrange("a r c -> r a c"), o[0:15, :, :], [ok_], ["pool_own"], semkey=ok_)
        P.barrier()
        A.lo = lo_d
        pw = A.alloc([64, 128], BF16)
        osb = Rot([A.alloc([TC], BF16) for _ in range(2)], "osbB")
        k.dma("pool", pw, k.dr["pool_w_blk"], (), ["pw"], semkey="pw", cast=True)
        for g in range(4):
            for eb in range(4):
                bset[0] ^= 1
                base = 3 * bset[0]
                o, ok_ = osb.next()
                for pi, (a, b_) in enumerate(PIECES_TC):
                    pk = f"ps{base + pi}"
                    for cc in range(4):
                        k.mm(ps[base + pi][:, 0:b_ - a], pw[:, (g * 4 + eb) * 4 + cc, :], dT[:, g * 4 + cc, a:b_], cc == 0, cc == 3, ["pw"], [pk])
                    k.act(o[:, a:b_], ps[base + pi][:, 0:b_ - a], AF.Copy, [pk, "psc"], [ok_], scale=c["psc"][:, g * 4 + eb:g * 4 + eb + 1])
                k.dma("sp", k.dr["oT_s"][16 + g * 4 + eb], o[:, :], [ok_], ["oT_s"], semkey=ok_)
    P.barrier()
    A.lo = lo0
    if k.cfg.get("B1", True):
        wqs = A.alloc([8, 256], F32)
        wqb = Rot([A.alloc([8, 256], BF16) for _ in range(2)], "wqbB")
        cs = A.alloc([2, TE], F32)
        qo = Rot([A.alloc([TE], BF16) for _ in range(2)], "qoB")
        qpo = Rot([A.alloc([TE], BF16) for _ in range(2)], "qpoB")
        t1 = A.alloc([TE], F32)
        t2 = A.alloc([TE], F32)
        k.dma("sp", cs[0:64, :, :], k.dr["csTo"][:, :, :], (), ["csB4"], semkey="csB4")
        NQ = [f"nq{eb}" for eb in range(8)]
        for h in range(16):
            w, wk = wqb.next()
            load_w_block(k, w[:, :, :], k.dr["w_uq_blk"][h], 256, 8, c["gqc"], wqs, "wqsB", ["gqc"], [wk])
            P.op("act", lambda hh, w=w: hh.mul(out=w[:, :, 192:224], in_=w[:, :, 192:224], mul=-1.0), [wk], [wk])
            o, ok_ = qo.next()
            bset[0] ^= 1
            base = 3 * bset[0]
            for pi, (a, b_) in enumerate(PIECES_TE):
                pk = f"ps{base + pi}"
                for ch in range(8):
                    k.mm(ps[base + pi][:, 0:b_ - a], w[:, ch, 0:128], nqT[:, ch, a:b_], ch == 0, ch == 7, [wk] + NQ, [pk])
                k.act(o[:, a:b_], ps[base + pi][:, 0:b_ - a], AF.Copy, [pk], [ok_], scale=SCALE)
            k.dma("sp", k.dr["qT_s"][h][:, 0:1024].rearrange("p (j t) -> p j t", t=128),
                  o[:, 0:1152].rearrange("p (j t) -> p j t", t=144)[:, :, 16:144], [ok_], [], semkey=ok_)
            k.dma("sp", k.dr["qT_s"][h][:, 1024:1152], o[:, 1152:1280], [ok_], [], semkey=ok_)
            for i in range(2):
                bset[0] ^= 1
                base = 3 * bset[0]
                tt_ = t1 if i == 0 else t2
                for pi, (a, b_) in enumerate(PIECES_TE):
                    pk = f"ps{base + pi}"
                    for ch in range(8):
                        k.mm(ps[base + pi][0:64, 0:b_ - a], w[:, ch, 128 + 64 * i:192 + 64 * i], nqT[:, ch, a:b_], ch == 0, ch == 7, [wk] + NQ, [pk])
                    k.tt("dve", tt_[0:64, a:b_], ps[base + pi][0:64, 0:b_ - a], cs[0:64, i, a:b_], ALU.mult, [pk, "csB4"], [f"tq{i}"])
            k.tt("pool", t1[0:64, :], t1[0:64, :], t2[0:64, :], ALU.add, ["tq0", "tq1"], ["tq0"])
            o2, ok2 = qpo.next()
            k.act(o2[0:64, :], t1[0:64, :], AF.Copy, ["tq0"], [ok2], scale=SCALE)
            k.dma("sp", k.dr["qpT_s"][h][:, 0:1024].rearrange("p (j t) -> p j t", t=128),
                  o2[0:64, 0:1152].rearrange("p (j t) -> p j t", t=144)[:, :, 16:144], [ok2], [], semkey=ok2)
            k.dma("sp", k.dr["qpT_s"][h][:, 1024:1152], o2[0:64, 1152:1280], [ok2], [], semkey=ok2)
    P.barrier()
    A.lo, A.hi = lo0, hi0


def phase_attn(k):
    A, P, c, ps = k.A, k.P, k.c, k.ps
    lo0, hi0 = A.lo, A.hi
    NH = k.cfg.get("nheads", 16)
    ckv = A.alloc([4, 8192], BF16)
    kpe = A.alloc([8192], BF16)
    mk = A.alloc([8, 128], BF16)
    cks = A.alloc([2, 4, 1088], BF16)
    kps = A.alloc([2, 1088], BF16)
    KT = A.alloc([8192], BF16)
    V = A.alloc([64, 128], BF16)
    KTs = A.alloc([2, 1088], BF16)
    Vs = A.alloc([2, 9, 136], BF16)
    wkv = Rot([A.alloc([4, 256], BF16) for _ in range(2)], "wkvT")
    qT = Rot([A.alloc([TC], BF16) for _ in range(2)], "qTt")
    qpT = Rot([A.alloc([TC], BF16) for _ in range(2)], "qpTt")
    PT = Rot([A.alloc([512], BF16) for _ in range(6)], "PTt")
    ob = Rot([A.alloc([128], BF16) for _ in range(2)], "obt")
    rinv = Rot([A.alloc([1], F32) for _ in range(2)], "rinvt")
    rec = A.alloc([1024], F32)
    oT = Rot([A.alloc([TC], BF16) for _ in range(2)], "oTt")
    for ch in range(4):
        k.dma("sp", ckv[:, ch, :], k.dr["ckvT_s"][:, ch, :], (), [f"ckv{ch}"], semkey=f"ckv{ch}")
    CKV = [f"ckv{ch}" for ch in range(4)]
    k.dma("sp", kpe[0:64, :], k.dr["kpeT_s"][:, :], (), ["kpe"], semkey="kpe")
    k.dma("pool", mk, k.dr["maskT"], (), ["mk"], semkey="mk", cast=True)
    for b in range(2):
        k.dma("pool", cks[:, b, :, 0:1024], k.dr["cckvT"][b], (), [f"cks{b}"], semkey=f"cks{b}", cast=True)
        k.dma("pool", kps[0:64, b, 0:1024], k.dr["ckpeT"][b], (), [f"kps{b}"], semkey=f"kps{b}", cast=True)
        k.cp("dve", cks[:, b, :, 1024:1088], c["ckvTs"][:, :, 64 * b:64 * b + 64], [], [f"cksn{b}"])
        k.cp("dve", kps[0:64, b, 1024:1088], c["kpeTs"][0:64, 64 * b:64 * b + 64], [], [f"kpsn{b}"])
    P.op("pool", lambda h: h.memset(Vs[:, :, :, 128:129], 1.0), (), ["Vsones"])
    rk = [0]
    rs = [0]

    def bkv():
        rk[0] = (rk[0] + 1) % 2
        return rk[0]

    def bs_():
        rs[0] = (rs[0] + 1) % 2
        return 2 + rs[0]

    for h in range(NH):
        wk, wkk = wkv.next()
        q, qk = qT.next()
        qp, qpk = qpT.next()
        k.dma("pool", wk, k.dr["w_ukv_blk"][h], (), [wkk], semkey=wkk, cast=True)
        k.dma("sp", q[:, :], k.dr["qT_s"][h], (), [qk], semkey=qk)
        k.dma("sp", qp[0:64, :], k.dr["qpT_s"][h], (), [qpk], semkey=qpk)
        o_t, otk = oT.next()
        vv_ = k.dr["peer_v"].rearrange("(c p) d -> p c d", p=128)
        for i8 in range(8):
            ds_, cg_ = divmod(h * 8 + i8, 16)
            k.dma("pool", k.dr["u16_s"][h * 8 + i8], k.dr["uT_blk"][h * 8 + i8], (), [], semkey="u16", cast=True)
        for pc in range(16):
            b = bkv()
            for ch in range(4):
                k.mm(ps[b][:, :], wk[:, ch, 0:128], ckv[:, ch, pc * 512:(pc + 1) * 512], ch == 0, ch == 3, [wkk] + CKV, [f"ps{b}"])
            k.cp("act", KT[:, pc * 512:(pc + 1) * 512], ps[b][:, :], [f"ps{b}"], [f"KT{pc}"])
        for vg in range(16):
            b = bkv()
            for i in range(4):
                sb = vg * 4 + i
                for ch in range(4):
                    k.mm(ps[b][:, i * 128:(i + 1) * 128], ckv[:, ch, sb * 128:(sb + 1) * 128], wk[:, ch, 128:256], ch == 0, ch == 3, [wkk] + CKV, [f"ps{b}"])
            k.cp("dve", V[:, vg * 4:vg * 4 + 4, :], ps[b][:, :].rearrange("p (a e) -> p a e", e=128), [f"ps{b}"], [f"V{vg}"])
        for bb in range(2):
            SK = [f"cks{bb}", f"cksn{bb}"]
            for (a, b_) in ((0, 512), (512, 1024), (1024, 1088)):
                b = bkv()
                for ch in range(4):
                    k.mm(ps[b][:, 0:b_ - a], wk[:, ch, 0:128], cks[:, bb, ch, a:b_], ch == 0, ch == 3, [wkk] + SK, [f"ps{b}"])
                k.cp("act", KTs[:, bb, a:b_], ps[b][:, 0:b_ - a], [f"ps{b}"], [f"KTs{bb}"])
            for vg in range(3):
                b = bkv()
                blks = range(vg * 4, min(vg * 4 + 4, 9))
                for i, sb in enumerate(blks):
                    rows = 128 if sb < 8 else 64
                    for ch in range(4):
                        k.mm(ps[b][0:rows, i * 128:(i + 1) * 128], cks[:, bb, ch, sb * 128:sb * 128 + rows], wk[:, ch, 128:256], ch == 0, ch == 3,
                             [wkk] + SK, [f"ps{b}"])
                if vg < 2:
                    k.cp("dve", Vs[:, bb, vg * 4:vg * 4 + 4, 0:128], ps[b][:, :].rearrange("p (a e) -> p a e", e=128), [f"ps{b}"], [f"Vs{bb}"])
                else:
                    k.cp("dve", Vs[0:64, bb, 8, 0:128], ps[b][0:64, 0:128], [f"ps{b}"], [f"Vs{bb}"])
        steps = []
        for sb in range(64):
            g = sb // 8
            for (a, b_) in ([(128 * g, 512), (512, 1024)] if g < 4 else [(128 * g, 1024)]):
                steps.append((sb, a, b_))
        SKEW = 2

        def scores(i):
            sb, a, b_ = steps[i]
            n = b_ - a
            bs = i % 4
            k.mm(ps[bs][:, 0:n], KT[:, sb * 128:(sb + 1) * 128], q[:, a:b_], True, False, [f"KT{sb // 4}", qk], [f"ps{bs}"])
            k.mm(ps[bs][:, 0:n], kpe[0:64, sb * 128:(sb + 1) * 128], qp[0:64, a:b_], False, True, ["kpe", qpk], [f"ps{bs}"])

        for i in range(min(SKEW, len(steps))):
            scores(i)
        for i, (sb, a, b_) in enumerate(steps):
            if i + SKEW < len(steps):
                scores(i + SKEW)
            g = sb // 8
            n = b_ - a
            bs = i % 4
            p, pk = PT.next()
            k.act(p[:, 0:n], ps[bs][:, 0:n], AF.Exp, [f"ps{bs}"], [pk])
            if a == 128 * g:
                k.tt("pool", p[:, 0:128], p[:, 0:128], mk[:, sb % 8, :], ALU.mult, [pk, "mk"], [pk])
            ob_, oc = (4, a) if a < 512 else (5, a - 512)
            k.mm(ps[ob_][:, oc:oc + n], V[:, sb, :], p[:, 0:n], sb == 0, sb == 63, [pk, f"V{sb // 4}"], [f"ps{ob_}"], nocheck=True)
            k.mm(ps[ob_ + 2][:, oc:oc + n], c["onesb"][:, :], p[:, 0:n], sb == 0, sb == 63, [pk, "onesb"], [f"ps{ob_ + 2}"], nocheck=True)
        for i in range(2):
            P.op("dve", lambda hh, i=i: hh.reciprocal(out=rec[:, i * 512:(i + 1) * 512], in_=ps[6 + i][:, :]), [f"ps{6 + i}"], [f"rec{i}"])
            k.tt("dve", o_t[:, i * 512:(i + 1) * 512], ps[4 + i][:, :], rec[:, i * 512:(i + 1) * 512], ALU.mult, [f"ps{4 + i}", f"rec{i}"], [otk])

        def finish(bo, rows, oc0, pk_):
            rv, rvk = rinv.next()
            P.op("dve", lambda hh: hh.reciprocal(out=rv[0:rows, :], in_=ps[bo][0:rows, 128:129]), [pk_], [rvk])
            o, okk = ob.next()
            k.act(o[0:rows, :], ps[bo][0:rows, 0:128], AF.Copy, [pk_, rvk], [okk], scale=rv[0:rows, 0:1])
            b = bkv()
            pv = ps[b][:, 0:64].bitcast(BF16)
            k.tp(pv[:, 0:rows], o[0:rows, :], c["identb"][0:rows, 0:rows], [okk, "identb"], [f"ps{b}"])
            k.cp("dve", o_t[:, oc0:oc0 + rows], pv[:, 0:rows], [f"ps{b}"], [otk])

        for bb in range(2):
            qa = 1024 + 64 * bb
            bo = bkv()
            for vg in range(3):
                bs = bs_()
                blks = list(range(vg * 4, min(vg * 4 + 4, 9)))
                rows = 128 if vg < 2 else 64
                for i, sb in enumerate(blks):
                    k.mm(ps[bs][0:rows, i * 64:(i + 1) * 64], KTs[:, bb, sb * 128:sb * 128 + rows], q[:, qa:qa + 64], True, False, [f"KTs{bb}", qk], [f"ps{bs}"])
                    k.mm(ps[bs][0:rows, i * 64:(i + 1) * 64], kps[0:64, bb, sb * 128:sb * 128 + rows], qp[0:64, qa:qa + 64], False, True,
                         [f"kps{bb}", f"kpsn{bb}", qpk], [f"ps{bs}"])
                p, pk = PT.next()
                n = len(blks) * 64
                k.act(p[0:rows, 0:n], ps[bs][0:rows, 0:n], AF.Exp, [f"ps{bs}"], [pk])
                for i, sb in enumerate(blks):
                    k.mm(ps[bo][0:64, 0:129], p[0:rows, i * 64:(i + 1) * 64], Vs[0:rows, bb, sb, 0:129], sb == 0, sb == 8, [pk, f"Vs{bb}", "Vsones"], [f"ps{bo}"])
            finish(bo, 64, 1024 + 64 * bb, f"ps{bo}")
        k.dma("sp", k.dr["oT_s"][h], o_t[:, :], [otk], [], semkey=otk)
    P.barrier()
    A.lo, A.hi = lo0, hi0


def phase_oproj(k):
    A, P, c, ps = k.A, k.P, k.c, k.ps
    lo0, hi0 = A.lo, A.hi
    oTa = A.alloc([32, TC], BF16)
    wo = Rot([A.alloc([32, 512], BF16) for _ in range(2)], "woO")
    xs_ = Rot([A.alloc([512], F32) for _ in range(3)], "xsO")
    x1o = Rot([A.alloc([512], F32) for _ in range(3)], "x1oO")
    junk = A.alloc([512], BF16)
    ssp = A.alloc([72], F32)
    tmp = A.alloc([9], F32)
    for blk in range(32):
        k.dma("sp", oTa[:, blk, :], k.dr["oT_s"][blk], (), [f"oTa{blk}"], semkey=f"oTa{blk % 4}")
    wov = k.dr["w_o"].rearrange("(ch p) d -> p ch d", p=128)
    rb = 0
    for ds in range(8):
        w, wk = wo.next()
        k.dma("pool", w, wov[:, :, ds * 512:(ds + 1) * 512], (), [wk], semkey=wk, cast=True)
        for j in range(9):
            rb = (rb + 1) % 4
            for ch in range(32):
                k.mm(ps[rb][:, :], oTa[:, ch, j * 128:(j + 1) * 128], w[:, ch, :], ch == 0, ch == 31, [wk, f"oTa{ch}"], [f"ps{rb}"])
            x, xk = xs_.next()
            k.dma("sp", x[:, :], k.dr["xo"][j][:, ds * 512:(ds + 1) * 512], (), [xk], semkey=xk)
            o, ok_ = x1o.next()
            k.tt("dve", o[:, :], ps[rb][:, :], x[:, :], ALU.add, [f"ps{rb}", xk], [ok_])
            k.act(junk[:, :], o[:, :], AF.Square, [ok_], ["junkO", f"ssp{j}_{ds}"], accum_out=ssp[:, j * 8 + ds:j * 8 + ds + 1])
            k.dma("sp", k.dr["x1_s"][j][:, ds * 512:(ds + 1) * 512], o[:, :], [ok_], ["x1_s"], semkey=ok_)
    SS = [f"ssp{j}_{ds}" for j in range(9) for ds in range(8)]
    P.op("dve", lambda h: h.tensor_reduce(out=tmp[:, :], in_=ssp[:, :].rearrange("p (j d) -> p j d", d=8), axis=AX.X, op=ALU.add), SS, ["tmpO"])
    k.rsq(c["r2c"][:, :], tmp[:, :], 1.0 / D, ["tmpO"], ["r2c"])
    P.barrier()
    A.lo, A.hi = lo0, hi0


def phase_ln2(k):
    A, P, c, ps = k.A, k.P, k.c, k.ps
    h2T = k.h2T = A.alloc_top([32, TC], BF16)
    lo0 = A.lo
    xin = Rot([A.alloc([4096], F32) for _ in range(2)], "xinL")
    xb = Rot([A.alloc([4096], BF16) for _ in range(2)], "xbL")
    rb = 0
    for j in range(9):
        x, xk = xin.next()
        k.dma("sp", x[:, :], k.dr["x1_s"][j], (), [xk], semkey=xk)
        y, yk = xb.next()
        k.act(y[:, :], x[:, :], AF.Copy, [xk, "r2c"], [yk], scale=c["r2c"][:, j:j + 1])
        for g8 in range(8):
            rb = (rb + 1) % 8
            pv = ps[rb][:, 0:256].bitcast(BF16)
            for i in range(4):
                ch = g8 * 4 + i
                k.tp(pv[:, i * 128:(i + 1) * 128], y[:, ch * 128:(ch + 1) * 128], c["identb"][:, :], [yk, "identb"], [f"ps{rb}"])
            k.tt("dve", h2T[:, g8 * 4:g8 * 4 + 4, j * 128:(j + 1) * 128], pv[:, :].rearrange("p (a t) -> p a t", t=128),
                 c["g2c"][:, g8 * 4:g8 * 4 + 4].unsqueeze(2).to_broadcast([128, 4, 128]), ALU.mult, [f"ps{rb}", "g2c"], [f"h2T{j}_{g8}"])
    P.barrier()
    A.lo = lo0


def phase_peer_q(k):
    A, P, c, ps = k.A, k.P, k.c, k.ps
    h2T = k.h2T
    lo0 = A.lo
    qpT = A.alloc([16, TC], BF16)
    skT = A.alloc([2, 8, 128], BF16)
    lo1 = A.lo
    wq = Rot([A.alloc([32, 128], BF16) for _ in range(2)], "wqP")
    for s_ in range(2):
        k.dma("pool", skT[:, s_, :, :], k.dr["skT"][s_], (), [f"skT{s_}"], semkey=f"skT{s_}", cast=True)
    bset = 0
    for blk in range(16):
        w, wk = wq.next()
        k.dma("pool", w, k.dr["wq_blk"][blk], (), [wk], semkey=wk, cast=True)
        bset ^= 1
        base = 3 * bset
        for pi, (a, b_) in enumerate(PIECES_TC):
            for ch in range(32):
                k.mm(ps[base + pi][:, 0:b_ - a], w[:, ch, :], h2T[:, ch, a:b_], ch == 0, ch == 31, [wk], [f"ps{base + pi}"])
            k.cp("act" if pi != 1 else "dve", qpT[:, blk, a:b_], ps[base + pi][:, 0:b_ - a], [f"ps{base + pi}"], [f"qpT{blk}"])
    P.barrier()
    A.lo = lo1
    s_rot = Rot([A.alloc([16, 128], F32) for _ in range(2)], "ssbP")
    m8 = A.alloc([16, 16], F32)
    tmpr = A.alloc([128], F32)
    cand = A.alloc([256], F32)
    tmpc = A.alloc([256], F32)
    negM = A.alloc([8], F32)
    Z = A.alloc([8], F32)
    lnZ = A.alloc([8], F32)
    junk = A.alloc([16], F32)
    c16a = k.c16a = A.alloc_top([9, 8, 16], F32)
    biasa = k.biasa = A.alloc_top([9, 8], F32)
    for j in range(9):
        jc = slice(j * 128, (j + 1) * 128)
        s_sb, ssk = s_rot.next()
        c16 = c16a[:, j, :, :]
        for q4 in range(4):
            for i in range(4):
                blk = q4 * 4 + i
                k.mm(ps[4 + q4][:, i * 128:(i + 1) * 128], qpT[:, blk, jc], skT[:, blk % 2, blk // 2, :], True, True, [f"qpT{blk}", f"skT{blk % 2}"], [f"ps{4 + q4}"])
            k.cp("act" if q4 % 2 == 0 else "dve", s_sb[:, q4 * 4:q4 * 4 + 4, :], ps[4 + q4][:, :].rearrange("p (a n) -> p a n", n=128), [f"ps{4 + q4}"], [f"{ssk}_{q4}"])
        SSK = [f"{ssk}_{q4}" for q4 in range(4)]
        k.dma("sp", k.dr["ss_s"][j], s_sb, SSK, [], semkey=ssk)
        for blk in range(16):
            sk = f"{ssk}_{blk // 4}"
            P.op("dve", lambda h, blk=blk, s_sb=s_sb: h.max(out=m8[:, blk, 0:8], in_=s_sb[:, blk, :]), [sk], [f"m8_{blk}"])
            P.op("dve", lambda h, blk=blk, s_sb=s_sb: h.match_replace(out=tmpr[:, :], in_to_replace=m8[:, blk, 0:8], in_values=s_sb[:, blk, :], imm_value=-1e30),
                 [sk, f"m8_{blk}"], ["tmpr"])
            P.op("dve", lambda h, blk=blk: h.max(out=m8[:, blk, 8:16], in_=tmpr[:, :]), ["tmpr"], [f"m8_{blk}"])
        for h_ in range(8):
            k.tt("dve", cand[:, :].rearrange("p (a b) -> p a b", b=16), m8[:, 2 * h_, :].unsqueeze(2).to_broadcast([128, 16, 16]),
                 m8[:, 2 * h_ + 1, :].unsqueeze(1).to_broadcast([128, 16, 16]), ALU.add, [f"m8_{2 * h_}", f"m8_{2 * h_ + 1}"], ["cand"])
            P.op("dve", lambda h, h_=h_, c16=c16: h.max(out=c16[:, h_, 0:8], in_=cand[:, :]), ["cand"], [f"c16_{h_}"])
            P.op("dve", lambda h, h_=h_, c16=c16: h.match_replace(out=tmpc[:, :], in_to_replace=c16[:, h_, 0:8], in_values=cand[:, :], imm_value=-1e30),
                 ["cand", f"c16_{h_}"], ["tmpc"])
            P.op("dve", lambda h, h_=h_, c16=c16: h.max(out=c16[:, h_, 8:16], in_=tmpc[:, :]), ["tmpc"], [f"c16_{h_}"])
        C16 = [f"c16_{h_}" for h_ in range(8)]
        k.ts("dve", negM[:, :], c16[:, :, 0], -1.0, None, ALU.mult, None, C16, ["negM"])
        for h_ in range(8):
            k.act(junk[:, :], c16[:, h_, :], AF.Exp, [f"c16_{h_}", "negM"], ["junkP", f"Z{h_}"], bias=negM[:, h_:h_ + 1], accum_out=Z[:, h_:h_ + 1])
        k.act(lnZ[:, :], Z[:, :], AF.Ln, [f"Z{h_}" for h_ in range(8)], ["lnZ"])
        k.tt("dve", biasa[:, j, :], negM[:, :], lnZ[:, :], ALU.subtract, ["negM", "lnZ"], ["biasP"])
    P.barrier()
    A.lo = lo0
    IB = 16
    NCH = k.cfg.get("nch", 128)
    s_rot = Rot([A.alloc([16, 128], F32) for _ in range(2)], "ssbG")
    sig = Rot([A.alloc([IB * 128], F32) for _ in range(3)], "sigP")
    eb_ = Rot([A.alloc([IB * 128], BF16) for _ in range(3)], "ebP")
    gh = Rot([A.alloc([IB * 128], BF16) for _ in range(4)], "ghP")
    GTst = Rot([A.alloc([16, 128], BF16) for _ in range(2)], "GTstP")
    uT = Rot([A.alloc([32, 128], BF16) for _ in range(3)], "uTU")
    gt = Rot([A.alloc([TC], BF16) for _ in range(3)], "gtU")
    gl = Rot([A.alloc([TC], F32) for _ in range(2)], "glU")
    ao = Rot([A.alloc([TC], BF16) for _ in range(3)], "aoU")
    pcnt = [0]

    postq = []
    gq = []
    GSKEW, PSKEW = 2, 2

    def u_chunk(ch_, ib):
        w, wk = uT.next()
        k.dma("sp", w, k.dr["u16_s"][ch_], (), [wk], semkey=wk)
        g, gk = gt.next()
        k.dma("sp", g[:, :], k.dr["GT_s"][ch_], [f"GTib{ib}"], [gk], semkey=gk)
        l, lk = gl.next()
        o, ok_ = ao.next()
        bks = []
        for pi, (a, b_) in enumerate(PIECES_TC):
            pcnt[0] += 1
            bk = 4 + pcnt[0] % 4
            bks.append(bk)
            for ch in range(32):
                k.mm(ps[bk][:, 0:b_ - a], w[:, ch, :], h2T[:, ch, a:b_], ch == 0, ch == 31, [wk], [f"ps{bk}"])
            if pi == 2:
                def post(bks=tuple(bks)):
                    for pi2, (a2, b2) in enumerate(PIECES_TC):
                        k.act(l[:, a2:b2], ps[bks[pi2]][:, 0:b2 - a2], AF.Gelu, [f"ps{bks[pi2]}"], [f"{lk}_{pi2}"])
                    for pi2, (a2, b2) in enumerate(PIECES_TC):
                        k.tt("dve", o[:, a2:b2], l[:, a2:b2], g[:, a2:b2], ALU.mult, [f"{lk}_{pi2}", gk], [ok_])
                    k.dma("sp", k.dr["aT_s"][ch_], o[:, :], [ok_], [], semkey=ok_)
                postq.append([2, post])
            yield

    pend = []

    def pump():
        if postq:
            postq[0][0] -= 1
            if postq[0][0] <= 0:
                postq.pop(0)[1]()
        while pend:
            try:
                next(pend[0])
                return
            except StopIteration:
                pend.pop(0)

    nib = NCH // IB
    for ib in range(nib + 1):
        for j in range(9):
            if ib >= 1:
                for cc in range((IB * j) // 9, (IB * (j + 1)) // 9):
                    pend.append(u_chunk((ib - 1) * IB + cc, ib - 1))
            if ib < nib:
                jc = slice(j * 128, (j + 1) * 128)
                s_sb, ssk = s_rot.next()
                k.dma("sp", s_sb, k.dr["ss_s"][j], (), [ssk], semkey=ssk)
                for h_ in range(8):
                    sg, sgk = sig.next()
                    k.tt("pool", sg[:, :].rearrange("p (a b) -> p a b", b=128),
                         s_sb[:, 2 * h_, ib * IB:(ib + 1) * IB].unsqueeze(2).to_broadcast([128, IB, 128]),
                         s_sb[:, 2 * h_ + 1, :].unsqueeze(1).to_broadcast([128, IB, 128]), ALU.add, [ssk], [sgk])
                    e, ek = eb_.next()
                    k.act(e[:, :], sg[:, :], AF.Exp, [sgk], [ek], bias=biasa[:, j, h_:h_ + 1])
                    g_, gk = gh.next()
                    k.stt(g_[:, :], sg[:, :], c16a[:, j, h_, 15:16], e[:, :], ALU.is_ge, ALU.mult, [sgk, ek], [gk])

                    def gmm(g_=g_, gk=gk, h_=h_):
                        for cc in range(IB):
                            k.mm(ps[cc // 4][:, (cc % 4) * 128:(cc % 4 + 1) * 128], g_[:, cc * 128:(cc + 1) * 128], c["identb"][:, :], h_ == 0 and cc % 4 == 0, h_ == 7,
                                 [gk, "identb"], [f"ps{cc // 4}"], nocheck=True)
                    gq.append(gmm)
                    if len(gq) > GSKEW:
                        gq.pop(0)()
                    pump()
                while gq:
                    gq.pop(0)()
                st, stk = GTst.next()
                for b4 in range(IB // 4):
                    k.cp("act", st[:, b4 * 4:b4 * 4 + 4, :], ps[b4][:, :].rearrange("p (a t) -> p a t", t=128), [f"ps{b4}"], [stk])
                k.dma("sp", k.dr["GT_s"][ib * IB:(ib + 1) * IB, :, jc].rearrange("c e t -> e c t"), st[:, :, :], [stk], [f"GTib{ib}"], semkey=stk)
            else:
                while pend:
                    pump()
    while postq:
        postq.pop(0)[1]()
    P.barrier()
    A.lo, A.hi = lo0, k.A.nbytes


def phase_peer_v(k):
    A, P, c, ps = k.A, k.P, k.c, k.ps
    lo0, hi0 = A.lo, A.hi
    NCH = k.cfg.get("nch", 128)
    NG = NCH // 8
    vc = A.alloc([NCH, 512], BF16)
    asam = A.alloc([NCH, 128], BF16)
    ag = Rot([A.alloc([8, 1024], BF16) for _ in range(2)], "agV")
    xs_ = Rot([A.alloc([512], F32) for _ in range(2)], "xsV")
    x2o = Rot([A.alloc([512], F32) for _ in range(2)], "x2oV")
    junk = A.alloc([512], BF16)
    ssp = A.alloc([72], F32)
    tmp = A.alloc([9], F32)
    vv = k.dr["peer_v"].rearrange("(c p) d -> p c d", p=128)
    for cg in range(NG):
        k.dma("sp", asam[:, cg * 8:(cg + 1) * 8, :], k.dr["aT_s"][cg * 8:(cg + 1) * 8, :, 1024:1152].rearrange("c e t -> e c t"), (), [f"asam{cg}"], semkey=f"asam{cg}")

    def evac(j, ds, bank):
        x, xk = xs_.next()
        k.dma("sp", x[:, :], k.dr["x1_s"][j][:, ds * 512:(ds + 1) * 512], (), [xk], semkey=xk)
        o, ok_ = x2o.next()
        k.tt("dve", o[:, :], ps[bank][:, :], x[:, :], ALU.add, [f"ps{bank}", xk], [ok_])
        k.act(junk[:, :], o[:, :], AF.Square, [ok_], ["junkV", f"ssv{j}_{ds}"], accum_out=ssp[:, j * 8 + ds:j * 8 + ds + 1])
        k.dma("sp", k.dr["x2_s"][j][:, ds * 512:(ds + 1) * 512], o[:, :], [ok_], [], semkey=ok_)

    for ds in range(8):
        for cg in range(NG):
            k.dma("pool", vc[:, cg * 8:(cg + 1) * 8, :], vv[:, cg * 8:(cg + 1) * 8, ds * 512:(ds + 1) * 512], (), [f"vc{cg}"], semkey=f"vc{cg}", cast=True)
            a_, ak = ag.next()
            k.dma("sp", a_, k.dr["aT_s"][cg * 8:(cg + 1) * 8, :, 0:1024].rearrange("c e t -> e c t"), (), [ak], semkey=ak)
            for j in range(8):
                for ch in range(8):
                    k.mm(ps[j][:, :], a_[:, ch, j * 128:(j + 1) * 128], vc[:, cg * 8 + ch, :], cg == 0 and ch == 0, cg == NG - 1 and ch == 7,
                         [ak, f"vc{cg}"], [f"ps{j}"])
        evac(7, ds, 7)
        for cg in range(NG):
            for ch in range(8):
                ci = cg * 8 + ch
                k.mm(ps[7][:, :], asam[:, ci, :], vc[:, ci, :], ci == 0, ci == NCH - 1, [f"asam{cg}", f"vc{cg}"], ["ps7"])
        for j in range(7):
            evac(j, ds, j)
        evac(8, ds, 7)
    SS = [f"ssv{j}_{ds}" for j in range(9) for ds in range(8)]
    P.op("dve", lambda h: h.tensor_reduce(out=tmp[:, :], in_=ssp[:, :].rearrange("p (j d) -> p j d", d=8), axis=AX.X, op=ALU.add), SS, ["tmpV"])
    k.rsq(c["r3c"][:, :], tmp[:, :], 1.0 / D, ["tmpV"], ["r3c"])
    P.barrier()
    A.lo, A.hi = lo0, hi0


def phase_final(k):
    A, P, c, ps = k.A, k.P, k.c, k.ps
    gf = A.alloc([4096], F32)
    xin = Rot([A.alloc([4096], F32) for _ in range(2)], "xinF")
    yo = Rot([A.alloc([4096], F32) for _ in range(2)], "yoF")
    k.dma("sp", gf[:, :], k.dr["gf_bc"], (), ["gf"], semkey="gf")
    for j in range(9):
        x, xk = xin.next()
        k.dma("sp", x[:, :], k.dr["x2_s"][j], (), [xk], semkey=xk)
        y, yk = yo.next()
        k.stt(y[:, :], x[:, :], c["r3c"][:, j:j + 1], gf[:, :], ALU.mult, ALU.mult, [xk, "gf", "r3c"], [yk])
        k.dma("sp", k.dr["y_own"][j], y[:, :], [yk], ["y_own"], semkey=yk)
    P.barrier()


def build(cfg):
    k = B(cfg)
    nga = cfg.get("nga", 16)
    k.din("g1c", [128, 32]); k.din("g2c", [128, 32]); k.din("gqc", [128, 8]); k.din("gkvc", [128, 4]); k.din("psc", [128, 16])
    k.din("w_in_blk", [28, 128, 32, 128]); k.din("w_pe_blk", [2, 128, 32, 64])
    k.din("xTa", [16, 128, 32, 512]); k.din("csTa", [64, 2, 8192])
    k.din("xTo", [128, 32, TE]); k.din("csTo", [64, 2, TE]); k.din("inv0", [128, 4, 144])
    k.din("spT", [2, 128, 16, 15]); k.din("pool_w_blk", [128, 64, 128]); k.din("w_uq_blk", [16, 128, 8, 256])
    k.din("w_ukv_blk", [16, 128, 4, 256]); k.din("maskT", [128, 8, 128]); k.din("cckvT", [2, 128, 4, 1024]); k.din("ckpeT", [2, 64, 1024])
    k.din("w_o", [4096, 4096]); k.din("xo", [9, 128, 4096]); k.din("wq_blk", [16, 128, 32, 128]); k.din("skT", [2, 128, 8, 128])
    k.din("uT_blk", [128, 128, 32, 128]); k.din("peer_v", [16384, 4096]); k.din("gf_bc", [128, 4096])
    k.dout("y_own", [9, 128, 4096])
    k.dscr("x1_s", [9, 128, 4096], F32); k.dscr("x2_s", [9, 128, 4096], F32)
    k.dscr("GT_s", [128, 128, TC], BF16); k.dscr("aT_s", [128, 128, TC], BF16)
    k.dscr("v16_s", [8, 16, 128, 8, 512], BF16); k.dscr("ss_s", [9, 128, 16, 128], F32); k.dscr("u16_s", [128, 128, 32, 128], BF16)
    k.din("fin_src", [1, 64]); k.dscr("fin_s", [1, 64], F32)
    k.dout("ckv_own", [9, 128, 512]); k.dout("kpe_own", [9, 128, 64]); k.dout("pool_own", [3, 15, 2048])
    k.dscr("ckvT_s", [128, 4, 8192], BF16); k.dscr("kpeT_s", [64, 8192], BF16)
    k.dscr("oT_s", [32, 128, TC], BF16); k.dscr("qT_s", [16, 128, TC], BF16); k.dscr("qpT_s", [16, 64, TC], BF16)
    phase_consts(k)
    k.P.barrier()
    if cfg.get("A", True):
        phase_A(k)
    phase_B(k)
    k.P.barrier()
    if cfg.get("upto", 99) >= 2:
        phase_attn(k)
    if cfg.get("upto", 99) >= 3:
        phase_oproj(k)
        phase_ln2(k)
    if cfg.get("upto", 99) >= 4:
        phase_peer_q(k)
    if cfg.get("upto", 99) >= 5:
        phase_peer_v(k)
        phase_final(k)
    k.P.barrier()
    fin = k.P.op("sp", lambda h: h.dma_start(out=k.dr["fin_s"][0:1, 0:64], in_=k.dr["g1c"][0:1, 0:32].bitcast(U8)[0:1, 0:64] if False else k.dr["fin_src"][0:1, 0:64]),
                 (), (), dma=True, semkey="fin")
    cnt = k.P.emit(final_wait_ops=[fin])
    k.stats = (cnt, k.P.n_sems, len(k.P.ops))
    return k


def _rope_tabs(pos):
    inv = 10000.0 ** (-2.0 * np.arange(32, dtype=np.float32) / 64).astype(np.float32)
    ang = pos.astype(np.float32)[None, :] * np.concatenate([inv, inv])[:, None]
    return np.stack([np.cos(ang), np.sin(ang)], 1).astype(np.float32)


def host_shared(inp):
    s = {}
    s["g1c"] = np.ascontiguousarray(inp["ln1_g"][0].reshape(32, 128).T)
    s["g2c"] = np.ascontiguousarray(inp["ln2_g"][0].reshape(32, 128).T)
    s["gqc"] = np.ascontiguousarray(inp["q_norm_g"][0].reshape(8, 128).T)
    s["gkvc"] = np.ascontiguousarray(inp["kv_norm_g"][0].reshape(4, 128).T)
    s["psc"] = np.ascontiguousarray(inp["pool_scale"][0].reshape(16, 128).T)
    W = inp["w_in"][0]
    sel = np.r_[0:1536, 1600:3648]
    s["w_in_blk"] = np.ascontiguousarray(W[:, sel].reshape(32, 128, 28, 128).transpose(2, 1, 0, 3))
    pe = W[:, 1536:1600]
    pr = np.concatenate([pe[:, 32:64], pe[:, 0:32]], 1)
    s["w_pe_blk"] = np.ascontiguousarray(np.stack([pe, pr]).reshape(2, 32, 128, 64).transpose(0, 2, 1, 3))
    xp = inp["x_prompt"][0]
    s["xTa"] = np.ascontiguousarray(xp.reshape(16, 512, 32, 128).transpose(0, 3, 2, 1))
    s["csTa"] = _rope_tabs(np.arange(8192))
    s["pool_w_blk"] = np.ascontiguousarray(inp["pool_w"][0].reshape(4, 4, 128, 4, 128).transpose(2, 0, 3, 1, 4).reshape(128, 64, 128))
    Wq = inp["w_uq"][0]
    qb = np.concatenate([Wq[:, :, 0:128], Wq[:, :, 128:192], Wq[:, :, 160:192], Wq[:, :, 128:160]], 2)
    s["w_uq_blk"] = np.ascontiguousarray(qb.reshape(8, 128, 16, 256).transpose(2, 1, 0, 3))
    s["fin_src"] = np.zeros((1, 64), np.float32)
    s["w_o"] = inp["w_o"][0]
    s["wq_blk"] = np.ascontiguousarray(inp["peer_wq"][0].reshape(32, 128, 16, 128).transpose(2, 1, 0, 3))
    s["skT"] = np.ascontiguousarray(np.stack([inp["peer_sk1"][0].transpose(2, 0, 1), inp["peer_sk2"][0].transpose(2, 0, 1)]))
    s["uT_blk"] = np.ascontiguousarray(inp["peer_u"][0].reshape(128, 128, 32, 128).transpose(0, 3, 2, 1))
    s["peer_v"] = inp["peer_v"][0]
    s["gf_bc"] = np.ascontiguousarray(np.broadcast_to(inp["final_g"][None, :], (128, 4096)))
    s["w_ukv_blk"] = np.ascontiguousarray(inp["w_ukv"][0].transpose(1, 0, 2).reshape(16, 4, 128, 256).transpose(0, 2, 1, 3))
    return s


def host_core(c, inp):
    m = {}
    xp = inp["x_prompt"][0]
    xs = inp["x_sample"]
    cols, pos = [], []
    for j in range(8):
        s0 = 128 * (8 * j + c)
        if s0 >= 16:
            cols.append(xp[s0 - 16:s0])
        else:
            cols.append(np.zeros((16, D), np.float32))
        cols.append(xp[s0:s0 + 128])
        pos += list(range(s0 - 16, s0 + 128))
    cols += [xs[2 * c], xs[2 * c + 1]]
    pos += list(range(1024, 1088)) * 2
    xext = np.concatenate(cols, 0)
    m["xTo"] = np.ascontiguousarray(xext.T.reshape(32, 128, TE).transpose(1, 0, 2))
    m["xo"] = np.ascontiguousarray(np.stack([xp[128 * (8 * j + c):128 * (8 * j + c) + 128] for j in range(8)] + [np.concatenate([xs[2 * c], xs[2 * c + 1]], 0)]))
    m["csTo"] = _rope_tabs(np.maximum(np.array(pos), 0))
    inv0 = np.zeros((4, 144), np.float32)
    for g in range(4):
        w = 2 << g
        p0 = 128 * c - 16 + np.arange(144)
        inv0[g] = 1.0 / np.minimum(np.maximum(p0, 0) + 1, w)
    m["inv0"] = np.ascontiguousarray(np.broadcast_to(inv0[None], (128, 4, 144)))
    mk = np.zeros((128, 8, 128), np.float32)
    for kb in range(8):
        if kb < c:
            mk[:, kb, :] = 1.0
        elif kb == c:
            ss = np.arange(128)[:, None] // 64
            tt = np.arange(128)[None, :] // 64
            mk[:, kb, :] = (tt >= ss).astype(np.float32)
    m["maskT"] = mk
    m["cckvT"] = np.ascontiguousarray(np.stack([inp["cache_ckv"][0, 2 * c + b].T.reshape(4, 128, 1024).transpose(1, 0, 2) for b in range(2)]))
    m["ckpeT"] = np.ascontiguousarray(np.stack([inp["cache_kpe"][0, 2 * c + b].T for b in range(2)]))
    sp = inp["state_pool"][0]
    m["spT"] = np.ascontiguousarray(np.stack([sp[2 * c + b].T.reshape(16, 128, 15).transpose(1, 0, 2) for b in range(2)]))
    return m


def kernel(**inputs):
    inp = {k_: np.asarray(v) for k_, v in inputs.items()}
    k = build({})
    sh = host_shared(inp)
    maps = []
    for c in range(NCORES):
        m = dict(sh)
        m.update(host_core(c, inp))
        maps.append(m)
    res = run_bass_kernel_spmd(k.nc, maps, core_ids=list(range(NCORES)))
    R = res.results
    y_p = np.zeros((1, 8192, D), np.float32)
    y_s = np.zeros((16, 64, D), np.float32)
    ckv_p = np.zeros((1, 1, 8192, 512), np.float32)
    kpe_p = np.zeros((1, 1, 8192, 64), np.float32)
    ckv_s = np.zeros((1, 16, 64, 512), np.float32)
    kpe_s = np.zeros((1, 16, 64, 64), np.float32)
    pool_s = np.zeros((1, 16, 15, 2048), np.float32)
    for c in range(NCORES):
        y = np.asarray(R[c]["y_own"])
        ck = np.asarray(R[c]["ckv_own"])
        kp = np.asarray(R[c]["kpe_own"])
        po = np.asarray(R[c]["pool_own"])
        for j in range(8):
            s0 = 128 * (8 * j + c)
            y_p[0, s0:s0 + 128] = y[j]
            ckv_p[0, 0, s0:s0 + 128] = ck[j]
            kpe_p[0, 0, s0:s0 + 128] = kp[j]
        for b in range(2):
            y_s[2 * c + b] = y[8, 64 * b:64 * b + 64]
            ckv_s[0, 2 * c + b] = ck[8, 64 * b:64 * b + 64]
            kpe_s[0, 2 * c + b] = kp[8, 64 * b:64 * b + 64]
            pool_s[0, 2 * c + b] = po[1 + b]
    pool_p = np.asarray(R[7]["pool_own"])[0][None, None].astype(np.float32)
    return (y_p, y_s, ckv_p, kpe_p, pool_p, ckv_s, kpe_s, pool_s)
```
